# Optimizing a Trainium2 kernel written in Bass

```python
import jax
import jax.numpy as jnp
from jax import lax
import numpy as np

D_MODEL = 2048
BATCH = 8
SEQ = 2048
DEPTH = 1

CHUNK = 64
LEFT_CHUNKS = 8
BAND = (LEFT_CHUNKS + 1) * CHUNK
HEAD_DIM = 128
ATTN_WIDTH = D_MODEL // 2
ATTN_HEADS = ATTN_WIDTH // HEAD_DIM
MAX_REL = 128
GMLP_WIDTH = D_MODEL - ATTN_WIDTH
GMLP_GROUP_DIM = 128
GMLP_GROUPS = GMLP_WIDTH // GMLP_GROUP_DIM
SPATIAL = 128
MIX_WIDTH = ATTN_WIDTH + GMLP_WIDTH
IN_WIDTH = 3 * ATTN_WIDTH + 2 * GMLP_WIDTH
D_FF = 4 * D_MODEL
N_MOD = 6
EPS = 1e-6
NEG_INF = -1e30

kernel_name = 'streaming_hybrid_attn_gmlp_block'


def rms_norm(x, g):
    xf = x.astype(jnp.float32)
    y = xf * lax.rsqrt(jnp.mean(xf * xf, axis=-1, keepdims=True) + EPS)
    return (y * g.astype(jnp.float32)).astype(x.dtype)


def modulate(h, shift, scale):
    return h * (1 + scale[:, None, :]) + shift[:, None, :]


def chunk_band(t, nc):
    tp = jnp.pad(t, ((0, 0), (LEFT_CHUNKS, 0), (0, 0), (0, 0), (0, 0)))
    return jnp.concatenate([tp[:, i:i + nc] for i in range(LEFT_CHUNKS + 1)], axis=2)


def chunked_rel_attention(q, k, v, q_g, k_g, rel_bias):
    B, S = q.shape[0], q.shape[1]
    nc = S // CHUNK
    q = rms_norm(q, q_g).reshape(B, nc, CHUNK, ATTN_HEADS, HEAD_DIM)
    k = chunk_band(rms_norm(k, k_g).reshape(B, nc, CHUNK, ATTN_HEADS, HEAD_DIM), nc)
    v = chunk_band(v.reshape(B, nc, CHUNK, ATTN_HEADS, HEAD_DIM), nc)
    scores = jnp.einsum('bcqhd,bckhd->bhcqk', q, k).astype(jnp.float32) * (HEAD_DIM ** -0.5)
    qi = jnp.arange(CHUNK)[:, None]
    kj = jnp.arange(BAND)[None, :]
    rel = jnp.clip(qi + LEFT_CHUNKS * CHUNK - kj, -MAX_REL, MAX_REL) + MAX_REL
    bias = rel_bias.astype(jnp.float32)[:, rel]
    key_chunk = jnp.arange(nc)[:, None] - LEFT_CHUNKS + jnp.arange(BAND)[None, :] // CHUNK
    valid = key_chunk >= 0
    scores = jnp.where(valid[None, None, :, None, :], scores + bias[None, :, None], NEG_INF)
    probs = jax.nn.softmax(scores, axis=-1).astype(v.dtype)
    out = jnp.einsum('bhcqk,bckhd->bcqhd', probs, v)
    return out.reshape(B, S, ATTN_WIDTH)


def spatial_gating(u, v, v_g, w_s, b_s):
    B, S = u.shape[0], u.shape[1]
    nb = S // SPATIAL
    v = rms_norm(v.reshape(B, S, GMLP_GROUPS, GMLP_GROUP_DIM), v_g)
    v = v.reshape(B, nb, SPATIAL, GMLP_GROUPS, GMLP_GROUP_DIM)
    t = jnp.arange(SPATIAL)
    mask = (t[:, None] // CHUNK) >= (t[None, :] // CHUNK)
    w = jnp.where(mask[None], w_s, 0)
    y = jnp.einsum('gts,bnsgc->bntgc', w, v) + jnp.transpose(b_s)[:, :, None]
    out = u.reshape(B, nb, SPATIAL, GMLP_GROUPS, GMLP_GROUP_DIM) * y
    return out.reshape(B, S, GMLP_WIDTH)


def setup_inputs(seed: int = 0) -> dict:
    key = jax.random.key(seed)
    ks = jax.random.split(key, 18)
    f32 = jnp.float32
    L = DEPTH

    def nrm(k, shape, s):
        return jax.random.normal(k, shape, f32) * s

    return {
        'x': nrm(ks[0], (BATCH, SEQ, D_MODEL), 1.0),
        'c': nrm(ks[1], (BATCH, D_MODEL), 1.0),
        'w_ada': nrm(ks[2], (L, D_MODEL, N_MOD * D_MODEL), D_MODEL ** -0.5),
        'b_ada': nrm(ks[3], (L, N_MOD * D_MODEL), 0.01),
        'mix_norm_g': 1.0 + nrm(ks[4], (L, D_MODEL), 0.01),
        'w_in': nrm(ks[5], (L, D_MODEL, IN_WIDTH), D_MODEL ** -0.5),
        'q_norm_g': 1.0 + nrm(ks[6], (L, HEAD_DIM), 0.01),
        'k_norm_g': 1.0 + nrm(ks[7], (L, HEAD_DIM), 0.01),
        'rel_bias': nrm(ks[8], (L, ATTN_HEADS, 2 * MAX_REL + 1), 0.1),
        'gmlp_norm_g': 1.0 + nrm(ks[9], (L, GMLP_GROUPS, GMLP_GROUP_DIM), 0.01),
        'w_spatial': nrm(ks[10], (L, GMLP_GROUPS, SPATIAL, SPATIAL), SPATIAL ** -0.5),
        'b_spatial': 1.0 + nrm(ks[11], (L, GMLP_GROUPS, SPATIAL), 0.01),
        'attn_out_g': 1.0 + nrm(ks[12], (L, ATTN_WIDTH), 0.01),
        'gmlp_out_g': 1.0 + nrm(ks[13], (L, GMLP_WIDTH), 0.01),
        'w_out': nrm(ks[14], (L, MIX_WIDTH, D_MODEL), MIX_WIDTH ** -0.5),
        'ff_norm_g': 1.0 + nrm(ks[15], (L, D_MODEL), 0.01),
        'w_ff1': nrm(ks[16], (L, D_MODEL, D_FF), D_MODEL ** -0.5),
        'w_ff2': nrm(ks[17], (L, D_FF, D_MODEL), D_FF ** -0.5),
    }


def reference(x, c, w_ada, b_ada, mix_norm_g, w_in, q_norm_g, k_norm_g, rel_bias, gmlp_norm_g,
              w_spatial, b_spatial, attn_out_g, gmlp_out_g, w_out, ff_norm_g, w_ff1, w_ff2):
    B, S = x.shape[0], x.shape[1]
    head_shape = (B, S, ATTN_HEADS, HEAD_DIM)
    cond = jax.nn.silu(c)
    for l in range(DEPTH):
        mod = cond @ w_ada[l] + b_ada[l]
        shift_m, scale_m, gate_m, shift_f, scale_f, gate_f = jnp.split(mod, N_MOD, axis=-1)
        h = modulate(rms_norm(x, mix_norm_g[l]), shift_m, scale_m)
        proj = h @ w_in[l]
        q, k, v, u, z = jnp.split(
            proj, [ATTN_WIDTH, 2 * ATTN_WIDTH, 3 * ATTN_WIDTH, 3 * ATTN_WIDTH + GMLP_WIDTH], axis=-1)
        attn = chunked_rel_attention(q.reshape(head_shape), k.reshape(head_shape), v.reshape(head_shape),
                                     q_norm_g[l], k_norm_g[l], rel_bias[l])
        gm = spatial_gating(jax.nn.gelu(u, approximate=False), jax.nn.gelu(z, approximate=False),
                            gmlp_norm_g[l], w_spatial[l], b_spatial[l])
        mix = jnp.concatenate([rms_norm(attn, attn_out_g[l]), rms_norm(gm, gmlp_out_g[l])], axis=-1)
        x = x + gate_m[:, None, :] * (mix @ w_out[l])
        h = modulate(rms_norm(x, ff_norm_g[l]), shift_f, scale_f)
        x = x + gate_f[:, None, :] * (jnp.square(jax.nn.relu(h @ w_ff1[l])) @ w_ff2[l])
    return x
```

```python
import numpy as np
from contextlib import ExitStack
import concourse.bass as bass
import concourse.mybir as mybir
from concourse.bass_utils import run_bass_kernel_spmd

F32 = mybir.dt.float32
BF16 = mybir.dt.bfloat16
AF = mybir.ActivationFunctionType
ALU = mybir.AluOpType
AX = mybir.AxisListType

D = 2048
S = 2048
NCORES = 8
T = 512
NT = S // T
EPS = 1e-6
NSLOT = 6
NEG = -30000.0


class Buf:
    __slots__ = ("name", "last_w", "readers", "bank")

    def __init__(self, name, bank=False):
        self.name = name
        self.last_w = None
        self.readers = {}
        self.bank = bank


class DSem:
    __slots__ = ("key", "count")

    def __init__(self, key):
        self.key = key
        self.count = 0


class Prog:
    ENGS = ("pe", "act", "dve", "pool", "sp")

    def __init__(self, nc, same_engine_sync=True):
        self.nc = nc
        self.ops = {e: [] for e in self.ENGS}
        self.cnt = {e: 0 for e in self.ENGS}
        self.waited = {e: {} for e in self.ENGS}
        self.semkeys = list(self.ENGS)
        self.same_engine_sync = same_engine_sync
        self.final_tokens = []

    def dsem(self, name):
        key = "d_%s_%d" % (name, len(self.semkeys))
        self.semkeys.append(key)
        return DSem(key)

    def _deps(self, eng, reads, writes):
        deps = {}

        def add(tok):
            if tok is None:
                return
            k, v = tok
            if deps.get(k, 0) < v:
                deps[k] = v
        for b in reads:
            add(b.last_w)
            if b.bank:
                for k, v in b.readers.items():
                    if k != eng:
                        add((k, v))
        for b in writes:
            add(b.last_w)
            for k, v in b.readers.items():
                add((k, v))
        waits = []
        w = self.waited[eng]
        for k, v in deps.items():
            if k == eng and (eng == "pe" or not self.same_engine_sync):
                continue
            if w.get(k, 0) >= v:
                continue
            w[k] = v
            waits.append((k, v))
        return waits

    def _mark(self, tok, reads, writes):
        k, v = tok
        for b in writes:
            b.last_w = tok
            b.readers = {}
        for b in reads:
            if b.readers.get(k, 0) < v:
                b.readers[k] = v

    def op(self, eng, fn, reads=(), writes=()):
        waits = self._deps(eng, reads, writes)
        self.cnt[eng] += 1
        tok = (eng, self.cnt[eng])
        self.ops[eng].append((waits, fn, (eng, 1)))
        self._mark(tok, reads, writes)
        return tok

    def dma(self, queue, fn, dsem, reads=(), writes=(), final=False):
        waits = self._deps(queue, reads, writes)
        dsem.count += 1
        tok = (dsem.key, 16 * dsem.count)
        self.ops[queue].append((waits, fn, (dsem.key, 16)))
        self._mark(tok, reads, writes)
        if final:
            self.final_tokens.append(tok)
        return tok

    def emit(self):
        nc = self.nc
        with ExitStack() as st:
            sems = {}
            for k in self.semkeys:
                sems[k] = st.enter_context(nc.semaphore(k))
            fin = {}
            for k, v in self.final_tokens:
                fin[k] = max(fin.get(k, 0), v)
            block = st.enter_context(nc.Block())
            handles = {"pe": block.tensor, "act": block.scalar, "dve": block.vector,
                       "pool": block.gpsimd, "sp": block.sync}

            def make(engname):
                oplist = self.ops[engname]

                def body(e):
                    for waits, fn, inc in oplist:
                        for k, v in waits:
                            e.wait_ge(sems[k], v)
                        ins = fn(e)
                        ins.then_inc(sems[inc[0]], inc[1])
                    if engname == "sp":
                        for k, v in fin.items():
                            e.wait_ge(sems[k], v)
                return body
            for engname in self.ENGS:
                handles[engname](make(engname))


class _DryProg:
    def dsem(self, name):
        return DSem(name)

    def op(self, *a, **k):
        return None

    def dma(self, *a, **k):
        return None


def build_nc(ntiles=NT, same_engine_sync=True):
    nc = bass.Bass("TRN2", target_bir_lowering=False)

    def din(name, shape):
        return nc.dram_tensor(name, shape, F32, kind="ExternalInput").ap()
    x = din("x", [S, D])
    c_t = din("c_t", [128, 16])
    w_ada = din("w_ada", [D, 6 * D])
    b_ada = din("b_ada", [1, 6 * D])
    gmix_t = din("gmix_t", [128, 16])
    gff_t = din("gff_t", [128, 16])
    w_in = din("w_in", [D, 5120])
    gq_t = din("gq_t", [128, 1])
    gk_t = din("gk_t", [128, 1])
    biasT = din("biasT", [128, 8 * 5 * 128])
    vg_bc = din("vg_bc", [128, 1024])
    wsT = din("wsT", [128, 8 * 128])
    bs_t = din("bs_t", [128, 8])
    gout_t = din("gout_t", [128, 16])
    w_out = din("w_out", [D, D])
    w_ff1 = din("w_ff1", [D, 4 * D])
    w_ff2 = din("w_ff2", [4 * D, D])
    ident = din("ident", [128, 128])
    out = nc.dram_tensor("out", [S, D], F32, kind="ExternalOutput").ap()

    w_ada_v = w_ada.rearrange("(kc p) n -> p kc n", p=128)
    w_in_v = w_in.rearrange("(kc p) n -> p kc n", p=128)
    w_out_v = w_out.rearrange("(kc p) n -> p kc n", p=128)
    w_ff1_v = w_ff1.rearrange("(kc p) n -> p kc n", p=128)
    w_ff2_v = w_ff2.rearrange("(kc p) n -> p kc n", p=128)

    with ExitStack() as st:
        def sb(name, shape, dt):
            return st.enter_context(nc.sbuf_tensor(name, shape, dt))

        ident_f = sb("ident_f", [128, 128], F32)
        ident_b = sb("ident_b", [128, 128], BF16)
        ones_f = sb("ones_f", [1, 128], F32)
        eps_t = sb("eps_t", [128, 1], F32)
        c_sb = sb("c_sb", [128, 16], F32)
        cond_b = sb("cond_b", [128, 16], BF16)
        gmix_sb = sb("gmix_sb", [128, 16], F32)
        gff_sb = sb("gff_sb", [128, 16], F32)
        gout_sb = sb("gout_sb", [128, 16], F32)
        gq_sb = sb("gq_sb", [128, 1], F32)
        gk_sb = sb("gk_sb", [128, 1], F32)
        bs_sb = sb("bs_sb", [128, 8], F32)
        vg_sb = sb("vg_sb", [128, 1024], F32)
        bias_b = sb("bias_b", [128, 8 * 5 * 128], BF16)
        ws_b = sb("ws_b", [128, 8 * 128], BF16)
        AB = sb("AB", [128, 64], F32)
        gate_bc = [sb("gate_m_bc", [128, D], F32), sb("gate_f_bc", [128, D], F32)]
        rowbuf = [sb("rowbuf%d" % i, [1, 512], F32) for i in range(2)]
        x1 = [sb("x1_%d" % j, [128, D], F32) for j in range(4)]
        actT = sb("actT", [128, 16, T], BF16)
        arena = sb("arena", [128, 16, 512], BF16)
        kTr = sb("kTr", [128, 8, 8 * 128], BF16)
        vring = sb("vring", [128, 8, 8, 130], BF16)
        qT = sb("qT", [128, 8, T], BF16)
        mixj = [sb("mixj%d" % i, [128, D], BF16) for i in range(2)]
        xn = [sb("xn%d" % i, [128, D], BF16) for i in range(2)]
        f32s = sb("f32s", [128, D + 32], F32)
        xst1 = sb("xst1", [128, D], F32)
        NRT = 3
        rtmp = [sb("rtmp%d" % i, [128, 512], F32) for i in range(NRT)]
        eT = [sb("eT%d" % i, [128, 640], BF16) for i in range(2)]
        wslot = [sb("wslot%d" % i, [128, 4, 512], BF16) for i in range(NSLOT)]
        NSTAT = 12
        qkss = sb("qkss", [128, 64], F32)
        qkr = qkss
        stat = [sb("stat%d" % i, [128, 16], F32) for i in range(NSTAT)]
        attn_raw = f32s[:, 0:1040].rearrange("p (h d) -> p h d", h=8)
        gscr = xst1[:].rearrange("p (j n) -> p j n", j=2)

        PS = st.enter_context(nc.psum_tensor("PS", [128, 8, 512], F32))
        PT = [PS[:, 6, :].bitcast(BF16), PS[:, 7, :].bitcast(BF16)]

        def program(P, wsched, dry):
            b_const = Buf("const")
            b_constp = Buf("constp")
            b_c2 = Buf("const2")
            b_AB = [Buf("AB%d" % i) for i in range(4)]
            b_gate = [Buf("gate_m"), Buf("gate_f")]
            b_row = [Buf("row0"), Buf("row1")]
            b_x1 = [Buf("x1_%d" % j) for j in range(4)]
            b_actT = [Buf("actT%d" % j) for j in range(4)]
            b_arena = [Buf("arena%d" % g) for g in range(4)]
            b_kTr = [Buf("kTr%d" % s_) for s_ in range(8)]
            b_vr = [Buf("vr%d" % s_) for s_ in range(8)]
            b_qT = [Buf("qT%d" % j) for j in range(4)]
            b_mixj = [Buf("mixj0"), Buf("mixj1")]
            b_xn = [Buf("xn0"), Buf("xn1")]
            b_f32s = [Buf("f32s")]
            b_xst1 = Buf("xst1")
            b_qkss = Buf("qkss")
            b_rtmp = [Buf("rtmp%d" % i) for i in range(NRT)]
            b_eT = [Buf("eT0"), Buf("eT1")]
            b_ws = [Buf("wslot%d" % i) for i in range(NSLOT)]
            b_stat = [Buf("stat%d" % i) for i in range(NSTAT)]
            b_ps = [[Buf("ps%d" % i, bank=True)] for i in range(8)]

            s_const = P.dsem("const")
            s_constp = P.dsem("constp")
            s_x = [P.dsem("x%d" % j) for j in range(4)]
            s_xs = [P.dsem("xs0"), P.dsem("xs1")]
            s_o = [P.dsem("o%d" % j) for j in range(4)]
            s_ws = [P.dsem("ws%d" % i) for i in range(NSLOT)]
            s_row = [P.dsem("row0"), P.dsem("row1")]

            xst = [(f32s, b_f32s, s_xs[0]), (xst1, [b_xst1], s_xs[1])]

            ctr = {"stat": 0, "rtmp": 0, "ps": 0, "pt": 0, "row": 0, "xn": 0, "xst": 0}

            def nxt(key, n):
                i = ctr[key] % n
                ctr[key] += 1
                return i

            def new_stat():
                i = nxt("stat", NSTAT)
                return stat[i], b_stat[i]

            pinned = set()

            def new_banks(n):
                res = []
                while len(res) < n:
                    b = nxt("ps", 6)
                    if b not in pinned:
                        res.append(b)
                return res

            wstate = {"issued": 0, "consumed": 0}

            def w_issue_upto(i):
                while wstate["issued"] <= i and wstate["issued"] < len(wsched):
                    k = wstate["issued"]
                    s_ = k % NSLOT
                    src = wsched[k]
                    P.dma("pool", (lambda e, s_=s_, src=src: e.dma_start(out=wslot[s_][:], in_=src)),
                          s_ws[s_], writes=[b_ws[s_]])
                    wstate["issued"] += 1

            def w_next(src):
                i = wstate["consumed"]
                wstate["consumed"] += 1
                if dry:
                    wsched.append(src)
                else:
                    w_issue_upto(i + NSLOT - 1)
                s_ = i % NSLOT
                return wslot[s_], b_ws[s_]

            consts = [(ident_f[:], ident), (c_sb[:], c_t), (gmix_sb[:], gmix_t), (gff_sb[:], gff_t),
                      (gout_sb[:], gout_t), (gq_sb[:], gq_t), (gk_sb[:], gk_t), (bs_sb[:], bs_t),
                      (vg_sb[:], vg_bc)]
            for dst, src in consts:
                P.dma("sp", (lambda e, dst=dst, src=src: e.dma_start(out=dst, in_=src)), s_const)
            b_const.last_w = (s_const.key, 16 * s_const.count)
            P.dma("pool", lambda e: e.dma_start(out=bias_b[:], in_=biasT), s_constp)
            P.dma("pool", lambda e: e.dma_start(out=ws_b[:], in_=wsT), s_constp)
            b_constp.last_w = (s_constp.key, 16 * s_constp.count)

            P.op("dve", lambda e: e.memset(eps_t[:], EPS), writes=[b_c2])
            P.op("dve", lambda e: e.memset(ones_f[:], 1.0), writes=[b_c2])
            P.op("dve", lambda e: e.tensor_copy(out=ident_b[:], in_=ident_f[:]), reads=[b_const], writes=[b_c2])
            P.op("dve", lambda e: e.memset(vring[:, :, :, 128:130], 1.0), writes=b_vr)
            P.op("dve", lambda e: e.memset(ws_b[:].rearrange("p (g t) -> p g t", g=8)[64:128, :, 0:64], 0.0),
                 reads=[b_constp], writes=[b_constp])
            bias4 = bias_b[:].rearrange("p (h o q) -> p h o q", h=8, o=5)
            for o_ in (0, 1, 4):
                P.op("dve", (lambda e, o_=o_: e.tensor_tensor(out=bias4[:, :, o_, :], in0=bias4[:, :, o_, :], in1=bias4[:, :, 2, :],
                                                              op=ALU.subtract)), reads=[b_constp], writes=[b_constp])
            P.op("act", lambda e: e.activation(out=cond_b[:], in_=c_sb[:], func=AF.Silu), reads=[b_const], writes=[b_c2])
            P.op("dve", lambda e: e.tensor_scalar(out=gq_sb[:], in0=gq_sb[:], scalar1=float(128 ** -0.5), scalar2=None,
                                                  op0=ALU.mult), reads=[b_const], writes=[b_const])

            def rsqrt_from_ss(ss_ap, n_cols, inv_n, b_in):
                t, bt = new_stat()
                r, br = new_stat()
                P.op("act", lambda e: e.activation(out=t[:, 0:n_cols], in_=ss_ap, func=AF.Ln, scale=inv_n,
                                                   bias=eps_t[:, 0:1]), reads=[b_in, b_c2], writes=[bt])
                P.op("act", lambda e: e.activation(out=r[:, 0:n_cols], in_=t[:, 0:n_cols], func=AF.Exp, scale=-0.5),
                     reads=[bt], writes=[br])
                return r, br

            def ada_block(nb):
                ri = nxt("row", 2)
                P.dma("sp", (lambda e, ri=ri, nb=nb: e.dma_start(out=rowbuf[ri][:], in_=b_ada[0:1, nb * 512:(nb + 1) * 512])),
                      s_row[ri], writes=[b_row[ri]])
                bk = new_banks(1)[0]
                for g in range(4):
                    ws, bws = w_next(w_ada_v[:, g * 4:(g + 1) * 4, nb * 512:(nb + 1) * 512])

                    def f(e, ws=ws, g=g, bk=bk):
                        ins = None
                        for k4 in range(4):
                            kc = g * 4 + k4
                            ins = e.matmul(PS[0:1, bk, :], lhsT=cond_b[:, kc:kc + 1], rhs=ws[:, k4, :],
                                           start=(kc == 0), stop=(kc == 15))
                        return ins
                    P.op("pe", f, reads=[bws, b_c2], writes=b_ps[bk])
                P.op("dve", (lambda e, ri=ri, bk=bk: e.tensor_tensor(out=rowbuf[ri][:], in0=PS[0:1, bk, :], in1=rowbuf[ri][:],
                                                                      op=ALU.add)),
                     reads=b_ps[bk] + [b_row[ri]], writes=[b_row[ri]])
                return ri

            def ada_vec_pp(nb0, col0, bk):
                for i in range(4):
                    ri = ada_block(nb0 + i)

                    def f(e, ri=ri, i=i):
                        ins = None
                        for c in range(4):
                            col = col0 + i * 4 + c
                            ins = e.matmul(PS[:, bk, col:col + 1], lhsT=rowbuf[ri][0:1, c * 128:(c + 1) * 128],
                                           rhs=ones_f[0:1, 0:1], start=True, stop=True)
                        return ins
                    P.op("pe", f, reads=[b_row[ri], b_c2], writes=b_ps[bk])

            def ada_mod(which):
                nb_shift, nb_scale = (0, 4) if which == 0 else (12, 16)
                g_sb = gmix_sb if which == 0 else gff_sb
                bk = new_banks(1)[0]
                pinned.add(bk)
                ada_vec_pp(nb_shift, 0, bk)
                ada_vec_pp(nb_scale, 16, bk)
                pinned.discard(bk)
                a0 = which * 32
                P.op("dve", lambda e: e.scalar_tensor_tensor(out=AB[:, a0:a0 + 16], in0=PS[:, bk, 16:32], scalar=1.0,
                                                             in1=g_sb[:], op0=ALU.add, op1=ALU.mult),
                     reads=b_ps[bk] + [b_const], writes=[b_AB[which * 2]])
                P.op("dve", lambda e: e.tensor_copy(out=AB[:, a0 + 16:a0 + 32], in_=PS[:, bk, 0:16]),
                     reads=b_ps[bk], writes=[b_AB[which * 2 + 1]])

            def ada_gate(which):
                nb0 = 8 if which == 0 else 20
                for i in range(4):
                    ri = ada_block(nb0 + i)
                    bk = new_banks(1)[0]
                    P.op("pe", (lambda e, ri=ri, bk=bk: e.matmul(PS[:, bk, :], lhsT=ones_f[0:1, 0:128], rhs=rowbuf[ri][0:1, :],
                                                                 start=True, stop=True)),
                         reads=[b_row[ri], b_c2], writes=b_ps[bk])
                    P.op("dve", (lambda e, i=i, bk=bk: e.tensor_copy(out=gate_bc[which][:, i * 512:(i + 1) * 512], in_=PS[:, bk, :])),
                         reads=b_ps[bk], writes=[b_gate[which]])

            def AB_or(acol, kc):
                if acol == "gout":
                    return gout_sb[:, kc:kc + 1]
                return AB[:, acol + kc:acol + kc + 1]

            def transpose_half(src, b_src, j, half, acol, bcol, b_scal):
                pi = nxt("pt", 2)

                def f(e, pi=pi, half=half):
                    ins = None
                    for k in range(8):
                        kc = half * 8 + k
                        ins = e.transpose(out=PT[pi][:, k * 128:(k + 1) * 128], in_=src[:, kc * 128:(kc + 1) * 128],
                                          identity=ident_b[:])
                    return ins
                P.op("pe", f, reads=[b_src, b_c2], writes=b_ps[6 + pi])
                for k in range(8):
                    kc = half * 8 + k
                    dst = actT[:, kc, j * 128:(j + 1) * 128]
                    srcp = PT[pi][:, k * 128:(k + 1) * 128]
                    if pi == 0:
                        if bcol is None:
                            fn = (lambda e, dst=dst, srcp=srcp, kc=kc: e.activation(out=dst, in_=srcp, func=AF.Identity,
                                                                                    scale=AB_or(acol, kc)))
                        else:
                            fn = (lambda e, dst=dst, srcp=srcp, kc=kc: e.activation(out=dst, in_=srcp, func=AF.Identity,
                                                                                    scale=AB_or(acol, kc), bias=AB[:, bcol + kc:bcol + kc + 1]))
                        P.op("act", fn, reads=b_ps[6 + pi] + b_scal, writes=[b_actT[j]])
                    else:
                        if bcol is None:
                            fn = (lambda e, dst=dst, srcp=srcp, kc=kc: e.tensor_scalar(out=dst, in0=srcp, scalar1=AB_or(acol, kc),
                                                                                       scalar2=None, op0=ALU.mult))
                        else:
                            fn = (lambda e, dst=dst, srcp=srcp, kc=kc: e.tensor_scalar(out=dst, in0=srcp, scalar1=AB_or(acol, kc),
                                                                                       scalar2=AB[:, bcol + kc:bcol + kc + 1],
                                                                                       op0=ALU.mult, op1=ALU.add))
                        P.op("dve", fn, reads=b_ps[6 + pi] + b_scal, writes=[b_actT[j]])

            xn_pool = [(xn[0], b_xn[0]), (xn[1], b_xn[1]), (mixj[0], b_mixj[0]), (mixj[1], b_mixj[1])]

            def norm_elem(src_ap, b_src):
                ss, bss = new_stat()
                xnb, bxn = xn_pool[nxt("xn", 4)]
                P.op("act", lambda e: e.activation(out=xnb[:], in_=src_ap, func=AF.Square, accum_out=ss[:, 0:1]),
                     reads=b_src, writes=[bss, bxn])
                r, br = rsqrt_from_ss(ss[:, 0:1], 1, 1.0 / D, bss)
                P.op("dve", lambda e: e.tensor_scalar(out=xnb[:], in0=src_ap, scalar1=r[:, 0:1], scalar2=None, op0=ALU.mult),
                     reads=b_src + [br], writes=[bxn])
                return xnb, bxn

            def norm_transpose_gen(src_ap, b_src, j, a0):
                xnb, bxn = norm_elem(src_ap, b_src)
                yield
                scal = [b_AB[a0 // 32 * 2], b_AB[a0 // 32 * 2 + 1]]
                for half in range(2):
                    transpose_half(xnb, bxn, j, half, a0, a0 + 16, scal)
                    yield

            def norm_transpose(src_ap, b_src, j, a0):
                for _ in norm_transpose_gen(src_ap, b_src, j, a0):
                    pass

            def n1_units(it):
                t0 = it * T
                gens = []
                for j in range(4):
                    def g(j=j):
                        r0 = t0 + j * 128
                        xi = nxt("xst", 2)
                        buf, bb, sem = xst[xi]
                        P.dma("sp", (lambda e: e.dma_start(out=buf[:, 0:D], in_=x[r0:r0 + 128, :])), sem, writes=bb)
                        yield from norm_transpose_gen(buf[:, 0:D], bb, j, 0)
                    gens.append(g())
                return gens

            def run_norm_units(gens):
                for g in gens:
                    next(g, None)
                for g in gens:
                    for _ in g:
                        pass

            def n1(it):
                run_norm_units(n1_units(it))

            def mm_group(srcs, lhs_fn, b_lhs_fn, evac_fn, banks=None):
                if banks is None:
                    banks = new_banks(4)
                nk = len(srcs) * 4
                for g, src in enumerate(srcs):
                    ws, bws = w_next(src)
                    for j in range(4):
                        def f(e, ws=ws, g=g, j=j):
                            ins = None
                            for k4 in range(4):
                                kc = g * 4 + k4
                                ins = e.matmul(PS[:, banks[j], :], lhsT=lhs_fn(kc, j), rhs=ws[:, k4, :],
                                               start=(kc == 0), stop=(kc == nk - 1))
                            return ins
                        P.op("pe", f, reads=[bws] + b_lhs_fn(g, j), writes=b_ps[banks[j]])
                        if g == len(srcs) - 1:
                            evac_fn(j, banks[j])
                        yield

            def run(gen):
                for _ in gen:
                    pass

            def interleave(main, side, ratio):
                for _ in main:
                    for _ in range(ratio):
                        next(side, None)
                for _ in side:
                    pass

            def w_in_srcs(nb):
                return [w_in_v[:, g * 4:(g + 1) * 4, nb * 512:(nb + 1) * 512] for g in range(4)]

            qk = arena[:].rearrange("p (j a) n -> p j (a n)", a=4)
            uz = qk

            def tile_body(it):
                t0 = it * T
                for j in range(4):
                    r0 = t0 + j * 128
                    P.dma("sp", (lambda e, j=j, r0=r0: e.dma_start(out=x1[j][:], in_=x[r0:r0 + 128, :])), s_x[j], writes=[b_x1[j]])
                if it == 0:
                    n1(0)

                hT_fn = lambda kc, j: actT[:, kc, j * 128:(j + 1) * 128]
                hT_b = lambda g, j: [b_actT[j]]

                def evac_qk(nb):
                    def evac(j, bk):
                        dst = qk[:, j, nb * 512:(nb + 1) * 512]
                        if nb % 2 == 0:
                            P.op("act", lambda e: e.activation(out=dst, in_=PS[:, bk, :], func=AF.Copy),
                                 reads=b_ps[bk], writes=[b_arena[j]])
                        else:
                            P.op("dve", lambda e: e.tensor_copy(out=dst, in_=PS[:, bk, :]), reads=b_ps[bk], writes=[b_arena[j]])
                    return evac
                for nb in range(4):
                    run(mm_group(w_in_srcs(nb), hT_fn, hT_b, evac_qk(nb)))

                def evac_v(nb):
                    def evac(j, bk):
                        sl = (it * 4 + j) % 8
                        h0 = (nb - 4) * 4
                        dst = vring[:, sl, h0:h0 + 4, 0:128]
                        srcp = PS[:, bk, :].rearrange("p (h d) -> p h d", h=4)
                        P.op("act", lambda e: e.activation(out=dst, in_=srcp, func=AF.Copy), reads=b_ps[bk], writes=[b_vr[sl]])
                    return evac
                for j in range(4):
                    qkj = qk[:, j, :]
                    P.op("dve", (lambda e, qkj=qkj: e.tensor_tensor(out=f32s[:, 0:D], in0=qkj, in1=qkj, op=ALU.mult)),
                         reads=[b_arena[j]], writes=b_f32s)
                    P.op("dve", (lambda e, j=j: e.tensor_reduce(out=qkss[:, j * 16:(j + 1) * 16],
                                                                in_=f32s[:, 0:D].rearrange("p (h d) -> p h d", h=16),
                                                                axis=AX.X, op=ALU.add)),
                         reads=b_f32s, writes=[b_qkss])
                run(mm_group(w_in_srcs(4), hT_fn, hT_b, evac_v(4)))
                P.op("act", lambda e: e.activation(out=qkr[:], in_=qkss[:], func=AF.Ln, scale=1.0 / 128, bias=eps_t[:, 0:1]),
                     reads=[b_qkss, b_c2], writes=[b_qkss])
                P.op("act", lambda e: e.activation(out=qkr[:], in_=qkr[:], func=AF.Exp, scale=-0.5), reads=[b_qkss], writes=[b_qkss])
                for j in range(4):
                    qkj = qk[:, j, :]
                    P.op("dve", (lambda e, qkj=qkj, j=j: e.tensor_tensor(
                        out=qkj.rearrange("p (h d) -> p h d", h=16), in0=qkj.rearrange("p (h d) -> p h d", h=16),
                        in1=qkr[:, j * 16:(j + 1) * 16].unsqueeze(2).to_broadcast([128, 16, 128]), op=ALU.mult)),
                        reads=[b_arena[j], b_qkss], writes=[b_arena[j]])
                run(mm_group(w_in_srcs(5), hT_fn, hT_b, evac_v(5)))

                for j in range(4):
                    sl = (it * 4 + j) % 8
                    qkj = qk[:, j, :]
                    for half in range(2):
                        pi = nxt("pt", 2)

                        def f(e, pi=pi, half=half, qkj=qkj):
                            ins = None
                            for h in range(8):
                                c0 = half * 1024 + h * 128
                                ins = e.transpose(out=PT[pi][:, h * 128:(h + 1) * 128], in_=qkj[:, c0:c0 + 128],
                                                  identity=ident_b[:])
                            return ins
                        P.op("pe", f, reads=[b_arena[j], b_c2], writes=b_ps[6 + pi])
                        srcp = PT[pi].rearrange("p (h t) -> p h t", h=8)
                        dst = qT[:, :, j * 128:(j + 1) * 128] if half == 0 else kTr[:, :, sl * 128:(sl + 1) * 128]
                        gsc = gq_sb if half == 0 else gk_sb
                        bdst = [b_qT[j]] if half == 0 else [b_kTr[sl]]
                        if pi == 0:
                            P.op("act", (lambda e, dst=dst, srcp=srcp, gsc=gsc: e.activation(out=dst, in_=srcp, func=AF.Identity,
                                                                                             scale=gsc[:, 0:1])),
                                 reads=b_ps[6 + pi] + [b_const], writes=bdst)
                        else:
                            P.op("dve", (lambda e, dst=dst, srcp=srcp, gsc=gsc: e.tensor_scalar(out=dst, in0=srcp, scalar1=gsc[:, 0:1],
                                                                                                scalar2=None, op0=ALU.mult)),
                                 reads=b_ps[6 + pi] + [b_const], writes=bdst)

                def evac_uz(nb):
                    def evac(j, bk):
                        dst = uz[:, j, (nb - 6) * 512:(nb - 5) * 512]
                        P.op("act", lambda e: e.activation(out=dst, in_=PS[:, bk, :], func=AF.Gelu),
                             reads=b_ps[bk], writes=[b_arena[j]])
                    return evac
                for nb in range(6, 10):
                    run(mm_group(w_in_srcs(nb), hT_fn, hT_b, evac_uz(nb)))

                def attn_block(j):
                    gt = it * 4 + j
                    kbs = [kb for kb in range(gt - 4, gt + 1) if kb >= 0]
                    nb_k = len(kbs)
                    mi = j % 2

                    def scores(h):
                        s_ = h % 2

                        def sc_ap(i):
                            if i < 4:
                                return PS[:, 2 * s_, i * 128:(i + 1) * 128]
                            return PS[:, 2 * s_ + 1, 0:128]

                        def f(e):
                            ins = None
                            for i, kb in enumerate(kbs):
                                o = gt - kb
                                sl = kb % 8
                                nobias = o in (2, 3)
                                ins = e.matmul(sc_ap(i), lhsT=kTr[:, h, sl * 128:(sl + 1) * 128], rhs=qT[:, h, j * 128:(j + 1) * 128],
                                               start=True, stop=nobias)
                                if not nobias:
                                    b0 = (h * 5 + o) * 128
                                    ins = e.matmul(sc_ap(i), lhsT=ident_b[:], rhs=bias_b[:, b0:b0 + 128], start=False, stop=True)
                            return ins
                        P.op("pe", f, reads=[b_kTr[kb % 8] for kb in kbs] + [b_qT[j], b_constp, b_c2],
                             writes=b_ps[2 * s_] + b_ps[2 * s_ + 1])
                        n4 = min(nb_k, 4)
                        P.op("act", (lambda e: e.activation(out=eT[s_][:, 0:n4 * 128], in_=PS[:, 2 * s_, 0:n4 * 128], func=AF.Exp)),
                             reads=b_ps[2 * s_], writes=[b_eT[s_]])
                        if nb_k == 5:
                            P.op("act", (lambda e: e.activation(out=eT[s_][:, 512:640], in_=PS[:, 2 * s_ + 1, 0:128], func=AF.Exp)),
                                 reads=b_ps[2 * s_ + 1], writes=[b_eT[s_]])

                    def pv_(h):
                        s_ = h % 2
                        pv = PS[:, 2 * s_ + 1, 256:256 + 129]

                        def f2(e):
                            ins = None
                            for i, kb in enumerate(kbs):
                                sl = kb % 8
                                ins = e.matmul(pv, lhsT=eT[s_][:, i * 128:(i + 1) * 128], rhs=vring[:, sl, h, 0:129],
                                               start=(i == 0), stop=(i == len(kbs) - 1))
                            return ins
                        P.op("pe", f2, reads=[b_eT[s_]] + [b_vr[kb % 8] for kb in kbs], writes=b_ps[2 * s_ + 1])
                        P.op("act", (lambda e: e.activation(out=attn_raw[:, h, 0:129], in_=pv, func=AF.Copy)),
                             reads=b_ps[2 * s_ + 1], writes=b_f32s)
                    for h in range(9):
                        if h < 8:
                            scores(h)
                        if h >= 1:
                            pv_(h - 1)
                        yield
                    rd, brd = new_stat()
                    P.op("dve", (lambda e: e.reciprocal(out=rd[:, 0:8], in_=attn_raw[:, :, 128])), reads=b_f32s, writes=[brd])
                    P.op("dve", (lambda e: e.tensor_tensor(out=attn_raw[:, :, 0:128], in0=attn_raw[:, :, 0:128],
                                                           in1=rd[:, 0:8].unsqueeze(2).to_broadcast([128, 8, 128]), op=ALU.mult)),
                         reads=b_f32s + [brd], writes=b_f32s)
                    ss, bss = new_stat()
                    mix3 = mixj[mi][:, 0:1024].rearrange("p (h d) -> p h d", h=8)
                    P.op("act", (lambda e: e.activation(out=mix3, in_=attn_raw[:, :, 0:128], func=AF.Square, accum_out=ss[:, 0:1])),
                         reads=b_f32s, writes=[bss, b_mixj[mi]])
                    r, br = rsqrt_from_ss(ss[:, 0:1], 1, 1.0 / 1024, bss)
                    P.op("dve", (lambda e: e.tensor_scalar(out=mix3, in0=attn_raw[:, :, 0:128], scalar1=r[:, 0:1],
                                                           scalar2=None, op0=ALU.mult)),
                         reads=b_f32s + [br], writes=[b_mixj[mi]])
                    yield

                def gmlp_pair(j0):
                    zz = uz[:, j0:j0 + 2, 1024:2048]
                    gu = uz[:, j0:j0 + 2, 0:1024]
                    ba = [b_arena[j0], b_arena[j0 + 1]]
                    P.op("dve", (lambda e: e.tensor_tensor(out=gscr, in0=zz, in1=zz, op=ALU.mult)), reads=ba, writes=[b_xst1])
                    ss, bss = new_stat()
                    P.op("dve", (lambda e: e.tensor_reduce(out=ss[:, 0:16], in_=xst1[:].rearrange("p (h d) -> p h d", h=16),
                                                           axis=AX.X, op=ALU.add)), reads=[b_xst1], writes=[bss])
                    yield
                    yield
                    yield
                    r, br = rsqrt_from_ss(ss[:, 0:16], 16, 1.0 / 128, bss)
                    yield
                    P.op("dve", (lambda e: e.tensor_tensor(
                        out=xst1[:].rearrange("p (j h d) -> p j h d", j=2, h=8), in0=zz.rearrange("p j (h d) -> p j h d", h=8),
                        in1=r[:, 0:16].rearrange("p (j h) -> p j h", j=2).unsqueeze(3).to_broadcast([128, 2, 8, 128]), op=ALU.mult)),
                        reads=ba + [br], writes=[b_xst1])
                    yield
                    P.op("dve", (lambda e: e.tensor_tensor(out=zz, in0=gscr, in1=vg_sb[:].unsqueeze(1).to_broadcast([128, 2, 1024]),
                                                           op=ALU.mult)), reads=[b_xst1, b_const], writes=ba)
                    yield

                    def f3(e):
                        ins = None
                        for jj in range(2):
                            for g in range(8):
                                ins = e.matmul(PS[:, 4 + 2 * jj + g // 4, (g % 4) * 128:(g % 4 + 1) * 128],
                                               lhsT=ws_b[:, g * 128:(g + 1) * 128],
                                               rhs=uz[:, j0 + jj, 1024 + g * 128:1024 + (g + 1) * 128], start=True, stop=True)
                        return ins
                    P.op("pe", f3, reads=ba + [b_constp], writes=b_ps[4] + b_ps[5] + b_ps[6] + b_ps[7])
                    yield
                    for jj in range(2):
                        for hh in range(2):
                            P.op("dve", (lambda e, jj=jj, hh=hh: e.tensor_tensor(
                                out=gscr[:, jj, hh * 512:(hh + 1) * 512].rearrange("p (g c) -> p g c", g=4),
                                in0=PS[:, 4 + 2 * jj + hh, :].rearrange("p (g c) -> p g c", g=4),
                                in1=bs_sb[:, hh * 4:(hh + 1) * 4].unsqueeze(2).to_broadcast([128, 4, 128]), op=ALU.add)),
                                reads=b_ps[4 + 2 * jj + hh] + [b_const], writes=[b_xst1])
                    yield
                    P.op("dve", (lambda e: e.tensor_tensor(out=gscr, in0=gscr, in1=gu, op=ALU.mult)),
                         reads=[b_xst1] + ba, writes=[b_xst1])
                    yield
                    yield
                    yield
                    yield
                    ss2, bss2 = new_stat()
                    for jj in range(2):
                        mi = (j0 + jj) % 2
                        P.op("act", (lambda e, jj=jj, mi=mi: e.activation(out=mixj[mi][:, 1024:2048], in_=gscr[:, jj, :], func=AF.Square,
                                                                          accum_out=ss2[:, jj:jj + 1])),
                             reads=[b_xst1], writes=[bss2, b_mixj[mi]])
                    r2, br2 = rsqrt_from_ss(ss2[:, 0:2], 2, 1.0 / 1024, bss2)
                    yield
                    for jj in range(2):
                        mi = (j0 + jj) % 2
                        P.op("dve", (lambda e, jj=jj, mi=mi: e.tensor_scalar(out=mixj[mi][:, 1024:2048], in0=gscr[:, jj, :],
                                                                             scalar1=r2[:, jj:jj + 1], scalar2=None, op0=ALU.mult)),
                             reads=[b_xst1, br2], writes=[b_mixj[mi]])
                    yield

                def mix_T(j):
                    for half in range(2):
                        transpose_half(mixj[j % 2], b_mixj[j % 2], j, half, "gout", None, [b_const])
                        yield

                def chain(*gens):
                    for g in gens:
                        yield from g

                def interleave_every(main, side, every):
                    n = 0
                    for _ in main:
                        n += 1
                        if n % every == 0:
                            next(side, None)

                sideA = gmlp_pair(0)
                interleave_every(chain(attn_block(0), attn_block(1)), sideA, 1)
                run(sideA)
                sideB = chain(mix_T(0), mix_T(1), gmlp_pair(2))
                interleave_every(chain(attn_block(2), attn_block(3)), sideB, 1)
                run(sideB)
                run(chain(mix_T(2), mix_T(3)))

                if it == 0:
                    ada_gate(0)

                def evac_res(which, nb):
                    def evac(j, bk):
                        P.op("dve", (lambda e, bk=bk: e.tensor_tensor(out=PS[:, bk, :], in0=PS[:, bk, :],
                                                                      in1=gate_bc[which][:, nb * 512:(nb + 1) * 512], op=ALU.mult)),
                             reads=[b_gate[which]], writes=b_ps[bk])
                        P.op("dve", (lambda e, bk=bk, j=j: e.tensor_tensor(out=x1[j][:, nb * 512:(nb + 1) * 512],
                                                                           in0=PS[:, bk, :], in1=x1[j][:, nb * 512:(nb + 1) * 512], op=ALU.add)),
                             reads=b_ps[bk] + [b_x1[j]], writes=[b_x1[j]])
                    return evac
                for nb in range(4):
                    srcs = [w_out_v[:, g * 4:(g + 1) * 4, nb * 512:(nb + 1) * 512] for g in range(4)]
                    run(mm_group(srcs, hT_fn, hT_b, evac_res(0, nb)))

                if it == 0:
                    ada_mod(1)
                run_norm_units([norm_transpose_gen(x1[j][:], [b_x1[j]], j, 32) for j in range(4)])

                for hf in range(4):
                    for hb in range(4):
                        c0 = hf * 2048 + hb * 512
                        banks = new_banks(4)
                        for g in range(4):
                            ws, bws = w_next(w_ff1_v[:, g * 4:(g + 1) * 4, c0:c0 + 512])
                            for q4 in range(4):
                                def f(e, ws=ws, g=g, q4=q4, banks=banks):
                                    ins = None
                                    for k4 in range(4):
                                        kc = g * 4 + k4
                                        ins = e.matmul(PS[:, banks[q4], :], lhsT=ws[:, k4, q4 * 128:(q4 + 1) * 128], rhs=actT[:, kc, :],
                                                       start=(kc == 0), stop=(kc == 15))
                                    return ins
                                P.op("pe", f, reads=[bws] + b_actT, writes=b_ps[banks[q4]])
                        for q4 in range(4):
                            ti = nxt("rtmp", NRT)
                            bk = banks[q4]
                            P.op("act", (lambda e, ti=ti, bk=bk: e.activation(out=rtmp[ti][:], in_=PS[:, bk, :], func=AF.Relu)),
                                 reads=b_ps[bk], writes=[b_rtmp[ti]])
                            P.op("dve", (lambda e, ti=ti, hb=hb, q4=q4: e.tensor_tensor(out=arena[:, hb * 4 + q4, :], in0=rtmp[ti][:],
                                                                                        in1=rtmp[ti][:], op=ALU.mult)),
                                 reads=[b_rtmp[ti]], writes=[b_arena[hb]])
                    if it == 0 and hf == 0:
                        ada_gate(1)

                    def ff2_quarter(hf=hf):
                        for nb in range(4):
                            srcs = [w_ff2_v[:, hf * 16 + g * 4:hf * 16 + g * 4 + 4, nb * 512:(nb + 1) * 512] for g in range(4)]
                            yield from mm_group(srcs, lambda kc, j: arena[:, kc, j * 128:(j + 1) * 128], lambda g, j: [b_arena[g]],
                                                evac_res(1, nb))
                    if hf == 3 and it + 1 < ntiles:
                        g0, g1, g2, g3 = n1_units(it + 1)
                        order = [g0, g1, None, None, g0, g0, g2, g1, g1, g3, g2, g2, None, g3, g3]

                        def side():
                            for g in order:
                                if g is not None:
                                    next(g, None)
                                yield
                        sd = side()
                        interleave_every(ff2_quarter(), sd, 4)
                        run(sd)
                    else:
                        run(ff2_quarter())
                for j in range(4):
                    r0 = t0 + j * 128
                    P.dma("sp", (lambda e, j=j, r0=r0: e.dma_start(out=out[r0:r0 + 128, :], in_=x1[j][:])), s_o[j],
                          reads=[b_x1[j]], final=True)

            ada_mod(0)
            for it in range(ntiles):
                tile_body(it)
            if not dry:
                assert wstate["consumed"] == len(wsched), (wstate, len(wsched))

        wsched = []
        program(_DryProg(), wsched, True)
        P = Prog(nc, same_engine_sync=same_engine_sync)
        program(P, wsched, False)
        P.emit()
    return nc


def _pk(v):
    return np.ascontiguousarray(np.asarray(v, np.float32).reshape(-1, 128).T)


def _bias_table(rel_bias):
    k = np.arange(128)[:, None, None]
    o = np.arange(5)[None, :, None]
    q = np.arange(128)[None, None, :]
    rel = np.clip(q - k + 128 * o, -128, 128) + 128
    invalid = ((o == 0) & (q < 64) & (k >= 64)) | ((o == 4) & (q >= 64) & (k < 64))
    tab = np.asarray(rel_bias, np.float32)[:, rel]
    tab = np.where(invalid[None], np.float32(NEG), tab)
    return np.ascontiguousarray(tab.transpose(1, 0, 2, 3)).reshape(128, 8 * 5 * 128)


_NC_CACHE = {}


def kernel(x, c, w_ada, b_ada, mix_norm_g, w_in, q_norm_g, k_norm_g, rel_bias, gmlp_norm_g,
           w_spatial, b_spatial, attn_out_g, gmlp_out_g, w_out, ff_norm_g, w_ff1, w_ff2):
    f = lambda a: np.ascontiguousarray(np.asarray(a, dtype=np.float32))
    x = f(x)
    c = f(c)
    if "nc" not in _NC_CACHE:
        _NC_CACHE["nc"] = build_nc()
    nc = _NC_CACHE["nc"]
    shared = {
        "w_ada": f(w_ada[0]), "b_ada": f(b_ada[0]).reshape(1, -1),
        "gmix_t": _pk(mix_norm_g[0]), "gff_t": _pk(ff_norm_g[0]),
        "w_in": f(w_in[0]),
        "gq_t": f(q_norm_g[0]).reshape(128, 1), "gk_t": f(k_norm_g[0]).reshape(128, 1),
        "biasT": _bias_table(rel_bias[0]),
        "vg_bc": np.ascontiguousarray(np.broadcast_to(f(gmlp_norm_g[0]).reshape(1, 1024), (128, 1024))),
        "wsT": np.ascontiguousarray(f(w_spatial[0]).transpose(2, 0, 1)).reshape(128, 8 * 128),
        "bs_t": np.ascontiguousarray(f(b_spatial[0]).T),
        "gout_t": _pk(np.concatenate([f(attn_out_g[0]), f(gmlp_out_g[0])])),
        "w_out": f(w_out[0]), "w_ff1": f(w_ff1[0]), "w_ff2": f(w_ff2[0]),
        "ident": np.eye(128, dtype=np.float32),
    }
    in_maps = []
    for b in range(NCORES):
        m = dict(shared)
        m["x"] = x[b]
        m["c_t"] = _pk(c[b])
        in_maps.append(m)
    res = run_bass_kernel_spmd(nc, in_maps, core_ids=list(range(NCORES)))
    return np.stack([np.asarray(r["out"], dtype=np.float32) for r in res.results], axis=0)
```

```python
import numpy as np
from contextlib import ExitStack
import concourse.bass as bass
import concourse.mybir as mybir
from concourse.bass_utils import run_bass_kernel_spmd

F32 = mybir.dt.float32
BF16 = mybir.dt.bfloat16
AF = mybir.ActivationFunctionType
ALU = mybir.AluOpType
AX = mybir.AxisListType

D = 2048
S = 2048
NCORES = 8
T = 512
NT = S // T
EPS = 1e-6
NSLOT = 6
NEG = -30000.0


class Buf:
    __slots__ = ("name", "last_w", "readers", "bank")

    def __init__(self, name, bank=False):
        self.name = name
        self.last_w = None
        self.readers = {}
        self.bank = bank


class DSem:
    __slots__ = ("key", "count")

    def __init__(self, key):
        self.key = key
        self.count = 0


class Prog:
    ENGS = ("pe", "act", "dve", "pool", "sp")

    def __init__(self, nc, same_engine_sync=True):
        self.nc = nc
        self.ops = {e: [] for e in self.ENGS}
        self.cnt = {e: 0 for e in self.ENGS}
        self.waited = {e: {} for e in self.ENGS}
        self.semkeys = list(self.ENGS)
        self.same_engine_sync = same_engine_sync
        self.final_tokens = []

    def dsem(self, name):
        key = "d_%s_%d" % (name, len(self.semkeys))
        self.semkeys.append(key)
        return DSem(key)

    def _deps(self, eng, reads, writes):
        deps = {}

        def add(tok):
            if tok is None:
                return
            k, v = tok
            if deps.get(k, 0) < v:
                deps[k] = v
        for b in reads:
            add(b.last_w)
            if b.bank:
                for k, v in b.readers.items():
                    if k != eng:
                        add((k, v))
        for b in writes:
            add(b.last_w)
            for k, v in b.readers.items():
                add((k, v))
        waits = []
        w = self.waited[eng]
        for k, v in deps.items():
            if k == eng and (eng == "pe" or not self.same_engine_sync):
                continue
            if w.get(k, 0) >= v:
                continue
            w[k] = v
            waits.append((k, v))
        return waits

    def _mark(self, tok, reads, writes):
        k, v = tok
        for b in writes:
            b.last_w = tok
            b.readers = {}
        for b in reads:
            if b.readers.get(k, 0) < v:
                b.readers[k] = v

    def op(self, eng, fn, reads=(), writes=()):
        waits = self._deps(eng, reads, writes)
        self.cnt[eng] += 1
        tok = (eng, self.cnt[eng])
        self.ops[eng].append((waits, fn, (eng, 1)))
        self._mark(tok, reads, writes)
        return tok

    def dma(self, queue, fn, dsem, reads=(), writes=(), final=False):
        waits = self._deps(queue, reads, writes)
        dsem.count += 1
        tok = (dsem.key, 16 * dsem.count)
        self.ops[queue].append((waits, fn, (dsem.key, 16)))
        self._mark(tok, reads, writes)
        if final:
            self.final_tokens.append(tok)
        return tok

    def emit(self):
        nc = self.nc
        with ExitStack() as st:
            sems = {}
            for k in self.semkeys:
                sems[k] = st.enter_context(nc.semaphore(k))
            fin = {}
            for k, v in self.final_tokens:
                fin[k] = max(fin.get(k, 0), v)
            block = st.enter_context(nc.Block())
            handles = {"pe": block.tensor, "act": block.scalar, "dve": block.vector,
                       "pool": block.gpsimd, "sp": block.sync}

            def make(engname):
                oplist = self.ops[engname]

                def body(e):
                    for waits, fn, inc in oplist:
                        for k, v in waits:
                            e.wait_ge(sems[k], v)
                        ins = fn(e)
                        ins.then_inc(sems[inc[0]], inc[1])
                    if engname == "sp":
                        for k, v in fin.items():
                            e.wait_ge(sems[k], v)
                return body
            for engname in self.ENGS:
                handles[engname](make(engname))


class _DryProg:
    def dsem(self, name):
        return DSem(name)

    def op(self, *a, **k):
        return None

    def dma(self, *a, **k):
        return None


def build_nc(ntiles=NT, same_engine_sync=True):
    nc = bass.Bass("TRN2", target_bir_lowering=False)

    def din(name, shape):
        return nc.dram_tensor(name, shape, F32, kind="ExternalInput").ap()
    x = din("x", [S, D])
    c_t = din("c_t", [128, 16])
    w_ada = din("w_ada", [D, 6 * D])
    b_ada = din("b_ada", [1, 6 * D])
    gmix_t = din("gmix_t", [128, 16])
    gff_t = din("gff_t", [128, 16])
    w_in = din("w_in", [D, 5120])
    gq_t = din("gq_t", [128, 1])
    gk_t = din("gk_t", [128, 1])
    biasT = din("biasT", [128, 8 * 5 * 128])
    vg_bc = din("vg_bc", [128, 1024])
    wsT = din("wsT", [128, 8 * 128])
    bs_t = din("bs_t", [128, 8])
    gout_t = din("gout_t", [128, 16])
    w_out = din("w_out", [D, D])
    w_ff1 = din("w_ff1", [D, 4 * D])
    w_ff2 = din("w_ff2", [4 * D, D])
    ident = din("ident", [128, 128])
    out = nc.dram_tensor("out", [S, D], F32, kind="ExternalOutput").ap()

    w_ada_v = w_ada.rearrange("(kc p) n -> p kc n", p=128)
    w_in_v = w_in.rearrange("(kc p) n -> p kc n", p=128)
    w_out_v = w_out.rearrange("(kc p) n -> p kc n", p=128)
    w_ff1_v = w_ff1.rearrange("(kc p) n -> p kc n", p=128)
    w_ff2_v = w_ff2.rearrange("(kc p) n -> p kc n", p=128)

    with ExitStack() as st:
        def sb(name, shape, dt):
            return st.enter_context(nc.sbuf_tensor(name, shape, dt))

        ident_f = sb("ident_f", [128, 128], F32)
        ident_b = sb("ident_b", [128, 128], BF16)
        ones_f = sb("ones_f", [1, 128], F32)
        eps_t = sb("eps_t", [128, 1], F32)
        c_sb = sb("c_sb", [128, 16], F32)
        cond_b = sb("cond_b", [128, 16], BF16)
        gmix_sb = sb("gmix_sb", [128, 16], F32)
        gff_sb = sb("gff_sb", [128, 16], F32)
        gout_sb = sb("gout_sb", [128, 16], F32)
        gq_sb = sb("gq_sb", [128, 1], F32)
        gk_sb = sb("gk_sb", [128, 1], F32)
        bs_sb = sb("bs_sb", [128, 8], F32)
        vg_sb = sb("vg_sb", [128, 1024], F32)
        bias_b = sb("bias_b", [128, 8 * 5 * 128], BF16)
        ws_b = sb("ws_b", [128, 8 * 128], BF16)
        AB = sb("AB", [128, 64], F32)
        gate_bc = [sb("gate_m_bc", [128, D], F32), sb("gate_f_bc", [128, D], F32)]
        rowbuf = [sb("rowbuf%d" % i, [1, 512], F32) for i in range(2)]
        x1 = [sb("x1_%d" % j, [128, D], F32) for j in range(4)]
        actT = sb("actT", [128, 16, T], BF16)
        arena = sb("arena", [128, 16, 512], BF16)
        kTr = sb("kTr", [128, 8, 8 * 128], BF16)
        vring = sb("vring", [128, 8, 8, 130], BF16)
        qT = sb("qT", [128, 8, T], BF16)
        mixj = [sb("mixj%d" % i, [128, D], BF16) for i in range(2)]
        xn = [sb("xn%d" % i, [128, D], BF16) for i in range(2)]
        f32s = sb("f32s", [128, D + 32], F32)
        xst1 = sb("xst1", [128, D], F32)
        NRT = 3
        rtmp = [sb("rtmp%d" % i, [128, 512], F32) for i in range(NRT)]
        eT = [sb("eT%d" % i, [128, 640], BF16) for i in range(2)]
        wslot = [sb("wslot%d" % i, [128, 4, 512], BF16) for i in range(NSLOT)]
        NSTAT = 12
        qkss = sb("qkss", [128, 64], F32)
        qkr = qkss
        stat = [sb("stat%d" % i, [128, 16], F32) for i in range(NSTAT)]
        attn_raw = f32s[:, 0:1040].rearrange("p (h d) -> p h d", h=8)
        gscr = xst1[:].rearrange("p (j n) -> p j n", j=2)

        PS = st.enter_context(nc.psum_tensor("PS", [128, 8, 512], F32))
        PT = [PS[:, 6, :].bitcast(BF16), PS[:, 7, :].bitcast(BF16)]

        def program(P, wsched, dry):
            b_const = Buf("const")
            b_constp = Buf("constp")
            b_c2 = Buf("const2")
            b_AB = [Buf("AB%d" % i) for i in range(4)]
            b_gate = [Buf("gate_m"), Buf("gate_f")]
            b_row = [Buf("row0"), Buf("row1")]
            b_x1 = [Buf("x1_%d" % j) for j in range(4)]
            b_actT = [Buf("actT%d" % j) for j in range(4)]
            b_arena = [Buf("arena%d" % g) for g in range(4)]
            b_kTr = [Buf("kTr%d" % s_) for s_ in range(8)]
            b_vr = [Buf("vr%d" % s_) for s_ in range(8)]
            b_qT = [Buf("qT%d" % j) for j in range(4)]
            b_mixj = [Buf("mixj0"), Buf("mixj1")]
            b_xn = [Buf("xn0"), Buf("xn1")]
            b_f32s = [Buf("f32s")]
            b_xst1 = Buf("xst1")
            b_qkss = Buf("qkss")
            b_rtmp = [Buf("rtmp%d" % i) for i in range(NRT)]
            b_eT = [Buf("eT0"), Buf("eT1")]
            b_ws = [Buf("wslot%d" % i) for i in range(NSLOT)]
            b_stat = [Buf("stat%d" % i) for i in range(NSTAT)]
            b_ps = [[Buf("ps%d" % i, bank=True)] for i in range(8)]

            s_const = P.dsem("const")
            s_constp = P.dsem("constp")
            s_x = [P.dsem("x%d" % j) for j in range(4)]
            s_xs = [P.dsem("xs0"), P.dsem("xs1")]
            s_o = [P.dsem("o%d" % j) for j in range(4)]
            s_ws = [P.dsem("ws%d" % i) for i in range(NSLOT)]
            s_row = [P.dsem("row0"), P.dsem("row1")]

            xst = [(f32s, b_f32s, s_xs[0]), (xst1, [b_xst1], s_xs[1])]

            ctr = {"stat": 0, "rtmp": 0, "ps": 0, "pt": 0, "row": 0, "xn": 0, "xst": 0}

            def nxt(key, n):
                i = ctr[key] % n
                ctr[key] += 1
                return i

            def new_stat():
                i = nxt("stat", NSTAT)
                return stat[i], b_stat[i]

            pinned = set()

            def new_banks(n):
                res = []
                while len(res) < n:
                    b = nxt("ps", 6)
                    if b not in pinned:
                        res.append(b)
                return res

            wstate = {"issued": 0, "consumed": 0}

            def w_issue_upto(i):
                while wstate["issued"] <= i and wstate["issued"] < len(wsched):
                    k = wstate["issued"]
                    s_ = k % NSLOT
                    src = wsched[k]
                    P.dma("pool", (lambda e, s_=s_, src=src: e.dma_start(out=wslot[s_][:], in_=src)),
                          s_ws[s_], writes=[b_ws[s_]])
                    wstate["issued"] += 1

            def w_next(src):
                i = wstate["consumed"]
                wstate["consumed"] += 1
                if dry:
                    wsched.append(src)
                else:
                    w_issue_upto(i + NSLOT - 1)
                s_ = i % NSLOT
                return wslot[s_], b_ws[s_]

            consts = [(ident_f[:], ident), (c_sb[:], c_t), (gmix_sb[:], gmix_t), (gff_sb[:], gff_t),
                      (gout_sb[:], gout_t), (gq_sb[:], gq_t), (gk_sb[:], gk_t), (bs_sb[:], bs_t),
                      (vg_sb[:], vg_bc)]
            for dst, src in consts:
                P.dma("sp", (lambda e, dst=dst, src=src: e.dma_start(out=dst, in_=src)), s_const)
            b_const.last_w = (s_const.key, 16 * s_const.count)
            P.dma("pool", lambda e: e.dma_start(out=bias_b[:], in_=biasT), s_constp)
            P.dma("pool", lambda e: e.dma_start(out=ws_b[:], in_=wsT), s_constp)
            b_constp.last_w = (s_constp.key, 16 * s_constp.count)

            P.op("dve", lambda e: e.memset(eps_t[:], EPS), writes=[b_c2])
            P.op("dve", lambda e: e.memset(ones_f[:], 1.0), writes=[b_c2])
            P.op("dve", lambda e: e.tensor_copy(out=ident_b[:], in_=ident_f[:]), reads=[b_const], writes=[b_c2])
            P.op("dve", lambda e: e.memset(vring[:, :, :, 128:130], 1.0), writes=b_vr)
            P.op("dve", lambda e: e.memset(ws_b[:].rearrange("p (g t) -> p g t", g=8)[64:128, :, 0:64], 0.0),
                 reads=[b_constp], writes=[b_constp])
            bias4 = bias_b[:].rearrange("p (h o q) -> p h o q", h=8, o=5)
            for o_ in (0, 1, 4):
                P.op("dve", (lambda e, o_=o_: e.tensor_tensor(out=bias4[:, :, o_, :], in0=bias4[:, :, o_, :], in1=bias4[:, :, 2, :],
                                                              op=ALU.subtract)), reads=[b_constp], writes=[b_constp])
            P.op("act", lambda e: e.activation(out=cond_b[:], in_=c_sb[:], func=AF.Silu), reads=[b_const], writes=[b_c2])
            P.op("dve", lambda e: e.tensor_scalar(out=gq_sb[:], in0=gq_sb[:], scalar1=float(128 ** -0.5), scalar2=None,
                                                  op0=ALU.mult), reads=[b_const], writes=[b_const])

            def rsqrt_from_ss(ss_ap, n_cols, inv_n, b_in):
                t, bt = new_stat()
                r, br = new_stat()
                P.op("act", lambda e: e.activation(out=t[:, 0:n_cols], in_=ss_ap, func=AF.Ln, scale=inv_n,
                                                   bias=eps_t[:, 0:1]), reads=[b_in, b_c2], writes=[bt])
                P.op("act", lambda e: e.activation(out=r[:, 0:n_cols], in_=t[:, 0:n_cols], func=AF.Exp, scale=-0.5),
                     reads=[bt], writes=[br])
                return r, br

            def ada_block(nb):
                ri = nxt("row", 2)
                P.dma("sp", (lambda e, ri=ri, nb=nb: e.dma_start(out=rowbuf[ri][:], in_=b_ada[0:1, nb * 512:(nb + 1) * 512])),
                      s_row[ri], writes=[b_row[ri]])
                bk = new_banks(1)[0]
                for g in range(4):
                    ws, bws = w_next(w_ada_v[:, g * 4:(g + 1) * 4, nb * 512:(nb + 1) * 512])

                    def f(e, ws=ws, g=g, bk=bk):
                        ins = None
                        for k4 in range(4):
                            kc = g * 4 + k4
                            ins = e.matmul(PS[0:1, bk, :], lhsT=cond_b[:, kc:kc + 1], rhs=ws[:, k4, :],
                                           start=(kc == 0), stop=(kc == 15))
                        return ins
                    P.op("pe", f, reads=[bws, b_c2], writes=b_ps[bk])
                P.op("dve", (lambda e, ri=ri, bk=bk: e.tensor_tensor(out=rowbuf[ri][:], in0=PS[0:1, bk, :], in1=rowbuf[ri][:],
                                                                      op=ALU.add)),
                     reads=b_ps[bk] + [b_row[ri]], writes=[b_row[ri]])
                return ri

            def ada_vec_pp(nb0, col0, bk):
                for i in range(4):
                    ri = ada_block(nb0 + i)

                    def f(e, ri=ri, i=i):
                        ins = None
                        for c in range(4):
                            col = col0 + i * 4 + c
                            ins = e.matmul(PS[:, bk, col:col + 1], lhsT=rowbuf[ri][0:1, c * 128:(c + 1) * 128],
                                           rhs=ones_f[0:1, 0:1], start=True, stop=True)
                        return ins
                    P.op("pe", f, reads=[b_row[ri], b_c2], writes=b_ps[bk])

            def ada_mod(which):
                nb_shift, nb_scale = (0, 4) if which == 0 else (12, 16)
                g_sb = gmix_sb if which == 0 else gff_sb
                bk = new_banks(1)[0]
                pinned.add(bk)
                ada_vec_pp(nb_shift, 0, bk)
                ada_vec_pp(nb_scale, 16, bk)
                pinned.discard(bk)
                a0 = which * 32
                P.op("dve", lambda e: e.scalar_tensor_tensor(out=AB[:, a0:a0 + 16], in0=PS[:, bk, 16:32], scalar=1.0,
                                                             in1=g_sb[:], op0=ALU.add, op1=ALU.mult),
                     reads=b_ps[bk] + [b_const], writes=[b_AB[which * 2]])
                P.op("dve", lambda e: e.tensor_copy(out=AB[:, a0 + 16:a0 + 32], in_=PS[:, bk, 0:16]),
                     reads=b_ps[bk], writes=[b_AB[which * 2 + 1]])

            def ada_gate(which):
                nb0 = 8 if which == 0 else 20
                for i in range(4):
                    ri = ada_block(nb0 + i)
                    bk = new_banks(1)[0]
                    P.op("pe", (lambda e, ri=ri, bk=bk: e.matmul(PS[:, bk, :], lhsT=ones_f[0:1, 0:128], rhs=rowbuf[ri][0:1, :],
                                                                 start=True, stop=True)),
                         reads=[b_row[ri], b_c2], writes=b_ps[bk])
                    P.op("dve", (lambda e, i=i, bk=bk: e.tensor_copy(out=gate_bc[which][:, i * 512:(i + 1) * 512], in_=PS[:, bk, :])),
                         reads=b_ps[bk], writes=[b_gate[which]])

            def AB_or(acol, kc):
                if acol == "gout":
                    return gout_sb[:, kc:kc + 1]
                return AB[:, acol + kc:acol + kc + 1]

            def transpose_half(src, b_src, j, half, acol, bcol, b_scal):
                pi = nxt("pt", 2)

                def f(e, pi=pi, half=half):
                    ins = None
                    for k in range(8):
                        kc = half * 8 + k
                        ins = e.transpose(out=PT[pi][:, k * 128:(k + 1) * 128], in_=src[:, kc * 128:(kc + 1) * 128],
                                          identity=ident_b[:])
                    return ins
                P.op("pe", f, reads=[b_src, b_c2], writes=b_ps[6 + pi])
                for k in range(8):
                    kc = half * 8 + k
                    dst = actT[:, kc, j * 128:(j + 1) * 128]
                    srcp = PT[pi][:, k * 128:(k + 1) * 128]
                    if pi == 0:
                        if bcol is None:
                            fn = (lambda e, dst=dst, srcp=srcp, kc=kc: e.activation(out=dst, in_=srcp, func=AF.Identity,
                                                                                    scale=AB_or(acol, kc)))
                        else:
                            fn = (lambda e, dst=dst, srcp=srcp, kc=kc: e.activation(out=dst, in_=srcp, func=AF.Identity,
                                                                                    scale=AB_or(acol, kc), bias=AB[:, bcol + kc:bcol + kc + 1]))
                        P.op("act", fn, reads=b_ps[6 + pi] + b_scal, writes=[b_actT[j]])
                    else:
                        if bcol is None:
                            fn = (lambda e, dst=dst, srcp=srcp, kc=kc: e.tensor_scalar(out=dst, in0=srcp, scalar1=AB_or(acol, kc),
                                                                                       scalar2=None, op0=ALU.mult))
                        else:
                            fn = (lambda e, dst=dst, srcp=srcp, kc=kc: e.tensor_scalar(out=dst, in0=srcp, scalar1=AB_or(acol, kc),
                                                                                       scalar2=AB[:, bcol + kc:bcol + kc + 1],
                                                                                       op0=ALU.mult, op1=ALU.add))
                        P.op("dve", fn, reads=b_ps[6 + pi] + b_scal, writes=[b_actT[j]])

            xn_pool = [(xn[0], b_xn[0]), (xn[1], b_xn[1]), (mixj[0], b_mixj[0]), (mixj[1], b_mixj[1])]

            def norm_elem(src_ap, b_src):
                ss, bss = new_stat()
                xnb, bxn = xn_pool[nxt("xn", 4)]
                P.op("act", lambda e: e.activation(out=xnb[:], in_=src_ap, func=AF.Square, accum_out=ss[:, 0:1]),
                     reads=b_src, writes=[bss, bxn])
                r, br = rsqrt_from_ss(ss[:, 0:1], 1, 1.0 / D, bss)
                P.op("dve", lambda e: e.tensor_scalar(out=xnb[:], in0=src_ap, scalar1=r[:, 0:1], scalar2=None, op0=ALU.mult),
                     reads=b_src + [br], writes=[bxn])
                return xnb, bxn

            def norm_transpose_gen(src_ap, b_src, j, a0):
                xnb, bxn = norm_elem(src_ap, b_src)
                yield
                scal = [b_AB[a0 // 32 * 2], b_AB[a0 // 32 * 2 + 1]]
                for half in range(2):
                    transpose_half(xnb, bxn, j, half, a0, a0 + 16, scal)
                    yield

            def norm_transpose(src_ap, b_src, j, a0):
                for _ in norm_transpose_gen(src_ap, b_src, j, a0):
                    pass

            def n1_units(it):
                t0 = it * T
                gens = []
                for j in range(4):
                    def g(j=j):
                        r0 = t0 + j * 128
                        xi = nxt("xst", 2)
                        buf, bb, sem = xst[xi]
                        P.dma("sp", (lambda e: e.dma_start(out=buf[:, 0:D], in_=x[r0:r0 + 128, :])), sem, writes=bb)
                        yield from norm_transpose_gen(buf[:, 0:D], bb, j, 0)
                    gens.append(g())
                return gens

            def run_norm_units(gens):
                for g in gens:
                    next(g, None)
                for g in gens:
                    for _ in g:
                        pass

            def n1(it):
                run_norm_units(n1_units(it))

            def mm_group(srcs, lhs_fn, b_lhs_fn, evac_fn, banks=None):
                if banks is None:
                    banks = new_banks(4)
                nk = len(srcs) * 4
                for g, src in enumerate(srcs):
                    ws, bws = w_next(src)
                    for j in range(4):
                        def f(e, ws=ws, g=g, j=j):
                            ins = None
                            for k4 in range(4):
                                kc = g * 4 + k4
                                ins = e.matmul(PS[:, banks[j], :], lhsT=lhs_fn(kc, j), rhs=ws[:, k4, :],
                                               start=(kc == 0), stop=(kc == nk - 1))
                            return ins
                        P.op("pe", f, reads=[bws] + b_lhs_fn(g, j), writes=b_ps[banks[j]])
                        if g == len(srcs) - 1:
                            evac_fn(j, banks[j])
                        yield

            def run(gen):
                for _ in gen:
                    pass

            def interleave(main, side, ratio):
                for _ in main:
                    for _ in range(ratio):
                        next(side, None)
                for _ in side:
                    pass

            def w_in_srcs(nb):
                return [w_in_v[:, g * 4:(g + 1) * 4, nb * 512:(nb + 1) * 512] for g in range(4)]

            qk = arena[:].rearrange("p (j a) n -> p j (a n)", a=4)
            uz = qk

            def tile_body(it):
                t0 = it * T
                for j in range(4):
                    r0 = t0 + j * 128
                    P.dma("sp", (lambda e, j=j, r0=r0: e.dma_start(out=x1[j][:], in_=x[r0:r0 + 128, :])), s_x[j], writes=[b_x1[j]])
                if it == 0:
                    for g in n1_first:
                        for _ in g:
                            pass

                hT_fn = lambda kc, j: actT[:, kc, j * 128:(j + 1) * 128]
                hT_b = lambda g, j: [b_actT[j]]

                def evac_qk(nb):
                    def evac(j, bk):
                        dst = qk[:, j, nb * 512:(nb + 1) * 512]
                        if nb % 2 == 0:
                            P.op("act", lambda e: e.activation(out=dst, in_=PS[:, bk, :], func=AF.Copy),
                                 reads=b_ps[bk], writes=[b_arena[j]])
                        else:
                            P.op("dve", lambda e: e.tensor_copy(out=dst, in_=PS[:, bk, :]), reads=b_ps[bk], writes=[b_arena[j]])
                    return evac
                for nb in range(4):
                    run(mm_group(w_in_srcs(nb), hT_fn, hT_b, evac_qk(nb)))

                def evac_v(nb):
                    def evac(j, bk):
                        sl = (it * 4 + j) % 8
                        h0 = (nb - 4) * 4
                        dst = vring[:, sl, h0:h0 + 4, 0:128]
                        srcp = PS[:, bk, :].rearrange("p (h d) -> p h d", h=4)
                        P.op("act", lambda e: e.activation(out=dst, in_=srcp, func=AF.Copy), reads=b_ps[bk], writes=[b_vr[sl]])
                    return evac
                for j in range(4):
                    qkj = qk[:, j, :]
                    P.op("dve", (lambda e, qkj=qkj: e.tensor_tensor(out=f32s[:, 0:D], in0=qkj, in1=qkj, op=ALU.mult)),
                         reads=[b_arena[j]], writes=b_f32s)
                    P.op("dve", (lambda e, j=j: e.tensor_reduce(out=qkss[:, j * 16:(j + 1) * 16],
                                                                in_=f32s[:, 0:D].rearrange("p (h d) -> p h d", h=16),
                                                                axis=AX.X, op=ALU.add)),
                         reads=b_f32s, writes=[b_qkss])
                run(mm_group(w_in_srcs(4), hT_fn, hT_b, evac_v(4)))
                P.op("act", lambda e: e.activation(out=qkr[:], in_=qkss[:], func=AF.Ln, scale=1.0 / 128, bias=eps_t[:, 0:1]),
                     reads=[b_qkss, b_c2], writes=[b_qkss])
                P.op("act", lambda e: e.activation(out=qkr[:], in_=qkr[:], func=AF.Exp, scale=-0.5), reads=[b_qkss], writes=[b_qkss])
                for j in range(4):
                    qkj = qk[:, j, :]
                    P.op("dve", (lambda e, qkj=qkj, j=j: e.tensor_tensor(
                        out=qkj.rearrange("p (h d) -> p h d", h=16), in0=qkj.rearrange("p (h d) -> p h d", h=16),
                        in1=qkr[:, j * 16:(j + 1) * 16].unsqueeze(2).to_broadcast([128, 16, 128]), op=ALU.mult)),
                        reads=[b_arena[j], b_qkss], writes=[b_arena[j]])
                run(mm_group(w_in_srcs(5), hT_fn, hT_b, evac_v(5)))

                for j in range(4):
                    sl = (it * 4 + j) % 8
                    qkj = qk[:, j, :]
                    for half in range(2):
                        pi = nxt("pt", 2)

                        def f(e, pi=pi, half=half, qkj=qkj):
                            ins = None
                            for h in range(8):
                                c0 = half * 1024 + h * 128
                                ins = e.transpose(out=PT[pi][:, h * 128:(h + 1) * 128], in_=qkj[:, c0:c0 + 128],
                                                  identity=ident_b[:])
                            return ins
                        P.op("pe", f, reads=[b_arena[j], b_c2], writes=b_ps[6 + pi])
                        srcp = PT[pi].rearrange("p (h t) -> p h t", h=8)
                        dst = qT[:, :, j * 128:(j + 1) * 128] if half == 0 else kTr[:, :, sl * 128:(sl + 1) * 128]
                        gsc = gq_sb if half == 0 else gk_sb
                        bdst = [b_qT[j]] if half == 0 else [b_kTr[sl]]
                        if pi == 0:
                            P.op("act", (lambda e, dst=dst, srcp=srcp, gsc=gsc: e.activation(out=dst, in_=srcp, func=AF.Identity,
                                                                                             scale=gsc[:, 0:1])),
                                 reads=b_ps[6 + pi] + [b_const], writes=bdst)
                        else:
                            P.op("dve", (lambda e, dst=dst, srcp=srcp, gsc=gsc: e.tensor_scalar(out=dst, in0=srcp, scalar1=gsc[:, 0:1],
                                                                                                scalar2=None, op0=ALU.mult)),
                                 reads=b_ps[6 + pi] + [b_const], writes=bdst)

                def evac_uz(nb):
                    def evac(j, bk):
                        dst = uz[:, j, (nb - 6) * 512:(nb - 5) * 512]
                        P.op("act", lambda e: e.activation(out=dst, in_=PS[:, bk, :], func=AF.Gelu),
                             reads=b_ps[bk], writes=[b_arena[j]])
                    return evac
                for nb in range(6, 10):
                    run(mm_group(w_in_srcs(nb), hT_fn, hT_b, evac_uz(nb)))

                def attn_block(j):
                    gt = it * 4 + j
                    kbs = [kb for kb in range(gt - 4, gt + 1) if kb >= 0]
                    nb_k = len(kbs)
                    mi = j % 2

                    def scores(h):
                        s_ = h % 2

                        def sc_ap(i):
                            if i < 4:
                                return PS[:, 2 * s_, i * 128:(i + 1) * 128]
                            return PS[:, 2 * s_ + 1, 0:128]

                        def f(e):
                            ins = None
                            for i, kb in enumerate(kbs):
                                o = gt - kb
                                sl = kb % 8
                                nobias = o in (2, 3)
                                ins = e.matmul(sc_ap(i), lhsT=kTr[:, h, sl * 128:(sl + 1) * 128], rhs=qT[:, h, j * 128:(j + 1) * 128],
                                               start=True, stop=nobias)
                                if not nobias:
                                    b0 = (h * 5 + o) * 128
                                    ins = e.matmul(sc_ap(i), lhsT=ident_b[:], rhs=bias_b[:, b0:b0 + 128], start=False, stop=True)
                            return ins
                        P.op("pe", f, reads=[b_kTr[kb % 8] for kb in kbs] + [b_qT[j], b_constp, b_c2],
                             writes=b_ps[2 * s_] + b_ps[2 * s_ + 1])
                        n4 = min(nb_k, 4)
                        P.op("act", (lambda e: e.activation(out=eT[s_][:, 0:n4 * 128], in_=PS[:, 2 * s_, 0:n4 * 128], func=AF.Exp)),
                             reads=b_ps[2 * s_], writes=[b_eT[s_]])
                        if nb_k == 5:
                            P.op("act", (lambda e: e.activation(out=eT[s_][:, 512:640], in_=PS[:, 2 * s_ + 1, 0:128], func=AF.Exp)),
                                 reads=b_ps[2 * s_ + 1], writes=[b_eT[s_]])

                    def pv_(h):
                        s_ = h % 2
                        pv = PS[:, 2 * s_ + 1, 256:256 + 129]

                        def f2(e):
                            ins = None
                            for i, kb in enumerate(kbs):
                                sl = kb % 8
                                ins = e.matmul(pv, lhsT=eT[s_][:, i * 128:(i + 1) * 128], rhs=vring[:, sl, h, 0:129],
                                               start=(i == 0), stop=(i == len(kbs) - 1))
                            return ins
                        P.op("pe", f2, reads=[b_eT[s_]] + [b_vr[kb % 8] for kb in kbs], writes=b_ps[2 * s_ + 1])
                        P.op("act", (lambda e: e.activation(out=attn_raw[:, h, 0:129], in_=pv, func=AF.Copy)),
                             reads=b_ps[2 * s_ + 1], writes=b_f32s)
                    for h in range(9):
                        if h < 8:
                            scores(h)
                        if h >= 1:
                            pv_(h - 1)
                        yield
                    rd, brd = new_stat()
                    P.op("dve", (lambda e: e.reciprocal(out=rd[:, 0:8], in_=attn_raw[:, :, 128])), reads=b_f32s, writes=[brd])
                    P.op("dve", (lambda e: e.tensor_tensor(out=attn_raw[:, :, 0:128], in0=attn_raw[:, :, 0:128],
                                                           in1=rd[:, 0:8].unsqueeze(2).to_broadcast([128, 8, 128]), op=ALU.mult)),
                         reads=b_f32s + [brd], writes=b_f32s)
                    ss, bss = new_stat()
                    mix3 = mixj[mi][:, 0:1024].rearrange("p (h d) -> p h d", h=8)
                    P.op("act", (lambda e: e.activation(out=mix3, in_=attn_raw[:, :, 0:128], func=AF.Square, accum_out=ss[:, 0:1])),
                         reads=b_f32s, writes=[bss, b_mixj[mi]])
                    r, br = rsqrt_from_ss(ss[:, 0:1], 1, 1.0 / 1024, bss)
                    P.op("dve", (lambda e: e.tensor_scalar(out=mix3, in0=attn_raw[:, :, 0:128], scalar1=r[:, 0:1],
                                                           scalar2=None, op0=ALU.mult)),
                         reads=b_f32s + [br], writes=[b_mixj[mi]])
                    yield

                def gmlp_pair(j0):
                    zz = uz[:, j0:j0 + 2, 1024:2048]
                    gu = uz[:, j0:j0 + 2, 0:1024]
                    ba = [b_arena[j0], b_arena[j0 + 1]]
                    P.op("dve", (lambda e: e.tensor_tensor(out=gscr, in0=zz, in1=zz, op=ALU.mult)), reads=ba, writes=[b_xst1])
                    ss, bss = new_stat()
                    P.op("dve", (lambda e: e.tensor_reduce(out=ss[:, 0:16], in_=xst1[:].rearrange("p (h d) -> p h d", h=16),
                                                           axis=AX.X, op=ALU.add)), reads=[b_xst1], writes=[bss])
                    yield
                    yield
                    yield
                    r, br = rsqrt_from_ss(ss[:, 0:16], 16, 1.0 / 128, bss)
                    yield
                    P.op("dve", (lambda e: e.tensor_tensor(
                        out=xst1[:].rearrange("p (j h d) -> p j h d", j=2, h=8), in0=zz.rearrange("p j (h d) -> p j h d", h=8),
                        in1=r[:, 0:16].rearrange("p (j h) -> p j h", j=2).unsqueeze(3).to_broadcast([128, 2, 8, 128]), op=ALU.mult)),
                        reads=ba + [br], writes=[b_xst1])
                    yield
                    P.op("dve", (lambda e: e.tensor_tensor(out=zz, in0=gscr, in1=vg_sb[:].unsqueeze(1).to_broadcast([128, 2, 1024]),
                                                           op=ALU.mult)), reads=[b_xst1, b_const], writes=ba)
                    yield

                    def f3(e):
                        ins = None
                        for jj in range(2):
                            for g in range(8):
                                ins = e.matmul(PS[:, 4 + 2 * jj + g // 4, (g % 4) * 128:(g % 4 + 1) * 128],
                                               lhsT=ws_b[:, g * 128:(g + 1) * 128],
                                               rhs=uz[:, j0 + jj, 1024 + g * 128:1024 + (g + 1) * 128], start=True, stop=True)
                        return ins
                    P.op("pe", f3, reads=ba + [b_constp], writes=b_ps[4] + b_ps[5] + b_ps[6] + b_ps[7])
                    yield
                    for jj in range(2):
                        for hh in range(2):
                            P.op("dve", (lambda e, jj=jj, hh=hh: e.tensor_tensor(
                                out=gscr[:, jj, hh * 512:(hh + 1) * 512].rearrange("p (g c) -> p g c", g=4),
                                in0=PS[:, 4 + 2 * jj + hh, :].rearrange("p (g c) -> p g c", g=4),
                                in1=bs_sb[:, hh * 4:(hh + 1) * 4].unsqueeze(2).to_broadcast([128, 4, 128]), op=ALU.add)),
                                reads=b_ps[4 + 2 * jj + hh] + [b_const], writes=[b_xst1])
                    yield
                    P.op("dve", (lambda e: e.tensor_tensor(out=gscr, in0=gscr, in1=gu, op=ALU.mult)),
                         reads=[b_xst1] + ba, writes=[b_xst1])
                    yield
                    yield
                    yield
                    yield
                    ss2, bss2 = new_stat()
                    for jj in range(2):
                        mi = (j0 + jj) % 2
                        P.op("act", (lambda e, jj=jj, mi=mi: e.activation(out=mixj[mi][:, 1024:2048], in_=gscr[:, jj, :], func=AF.Square,
                                                                          accum_out=ss2[:, jj:jj + 1])),
                             reads=[b_xst1], writes=[bss2, b_mixj[mi]])
                    r2, br2 = rsqrt_from_ss(ss2[:, 0:2], 2, 1.0 / 1024, bss2)
                    yield
                    for jj in range(2):
                        mi = (j0 + jj) % 2
                        P.op("dve", (lambda e, jj=jj, mi=mi: e.tensor_scalar(out=mixj[mi][:, 1024:2048], in0=gscr[:, jj, :],
                                                                             scalar1=r2[:, jj:jj + 1], scalar2=None, op0=ALU.mult)),
                             reads=[b_xst1, br2], writes=[b_mixj[mi]])
                    yield

                def mix_T(j):
                    for half in range(2):
                        transpose_half(mixj[j % 2], b_mixj[j % 2], j, half, "gout", None, [b_const])
                        yield

                def chain(*gens):
                    for g in gens:
                        yield from g

                def interleave_every(main, side, every):
                    n = 0
                    for _ in main:
                        n += 1
                        if n % every == 0:
                            next(side, None)

                sideA = gmlp_pair(0)
                interleave_every(chain(attn_block(0), attn_block(1)), sideA, 1)
                run(sideA)
                sideB = chain(mix_T(0), mix_T(1), gmlp_pair(2))
                interleave_every(chain(attn_block(2), attn_block(3)), sideB, 1)
                run(sideB)
                run(chain(mix_T(2), mix_T(3)))

                if it == 0:
                    ada_gate(0)

                def evac_res(which, nb):
                    def evac(j, bk):
                        P.op("dve", (lambda e, bk=bk: e.tensor_tensor(out=PS[:, bk, :], in0=PS[:, bk, :],
                                                                      in1=gate_bc[which][:, nb * 512:(nb + 1) * 512], op=ALU.mult)),
                             reads=[b_gate[which]], writes=b_ps[bk])
                        P.op("dve", (lambda e, bk=bk, j=j: e.tensor_tensor(out=x1[j][:, nb * 512:(nb + 1) * 512],
                                                                           in0=PS[:, bk, :], in1=x1[j][:, nb * 512:(nb + 1) * 512], op=ALU.add)),
                             reads=b_ps[bk] + [b_x1[j]], writes=[b_x1[j]])
                    return evac
                for nb in range(4):
                    srcs = [w_out_v[:, g * 4:(g + 1) * 4, nb * 512:(nb + 1) * 512] for g in range(4)]
                    run(mm_group(srcs, hT_fn, hT_b, evac_res(0, nb)))

                if it == 0:
                    ada_mod(1)
                run_norm_units([norm_transpose_gen(x1[j][:], [b_x1[j]], j, 32) for j in range(4)])

                for hf in range(4):
                    for hb in range(4):
                        c0 = hf * 2048 + hb * 512
                        banks = new_banks(4)
                        for g in range(4):
                            ws, bws = w_next(w_ff1_v[:, g * 4:(g + 1) * 4, c0:c0 + 512])
                            for q4 in range(4):
                                def f(e, ws=ws, g=g, q4=q4, banks=banks):
                                    ins = None
                                    for k4 in range(4):
                                        kc = g * 4 + k4
                                        ins = e.matmul(PS[:, banks[q4], :], lhsT=ws[:, k4, q4 * 128:(q4 + 1) * 128], rhs=actT[:, kc, :],
                                                       start=(kc == 0), stop=(kc == 15))
                                    return ins
                                P.op("pe", f, reads=[bws] + b_actT, writes=b_ps[banks[q4]])
                        for q4 in range(4):
                            ti = nxt("rtmp", NRT)
                            bk = banks[q4]
                            P.op("act", (lambda e, ti=ti, bk=bk: e.activation(out=rtmp[ti][:], in_=PS[:, bk, :], func=AF.Relu)),
                                 reads=b_ps[bk], writes=[b_rtmp[ti]])
                            P.op("dve", (lambda e, ti=ti, hb=hb, q4=q4: e.tensor_tensor(out=arena[:, hb * 4 + q4, :], in0=rtmp[ti][:],
                                                                                        in1=rtmp[ti][:], op=ALU.mult)),
                                 reads=[b_rtmp[ti]], writes=[b_arena[hb]])
                    if it == 0 and hf == 0:
                        ada_gate(1)

                    def ff2_quarter(hf=hf):
                        for nb in range(4):
                            srcs = [w_ff2_v[:, hf * 16 + g * 4:hf * 16 + g * 4 + 4, nb * 512:(nb + 1) * 512] for g in range(4)]
                            yield from mm_group(srcs, lambda kc, j: arena[:, kc, j * 128:(j + 1) * 128], lambda g, j: [b_arena[g]],
                                                evac_res(1, nb))
                    if hf == 3 and it + 1 < ntiles:
                        g0, g1, g2, g3 = n1_units(it + 1)
                        order = [g0, g1, None, None, g0, g0, g2, g1, g1, g3, g2, g2, None, g3, g3]

                        def side():
                            for g in order:
                                if g is not None:
                                    next(g, None)
                                yield
                        sd = side()
                        interleave_every(ff2_quarter(), sd, 4)
                        run(sd)
                    else:
                        run(ff2_quarter())
                for j in range(4):
                    r0 = t0 + j * 128
                    P.dma("sp", (lambda e, j=j, r0=r0: e.dma_start(out=out[r0:r0 + 128, :], in_=x1[j][:])), s_o[j],
                          reads=[b_x1[j]], final=True)

            n1_first = n1_units(0)
            for g in n1_first:
                next(g, None)
            ada_mod(0)
            for it in range(ntiles):
                tile_body(it)
            if not dry:
                assert wstate["consumed"] == len(wsched), (wstate, len(wsched))

        wsched = []
        program(_DryProg(), wsched, True)
        P = Prog(nc, same_engine_sync=same_engine_sync)
        program(P, wsched, False)
        P.emit()
    return nc


def _pk(v):
    return np.ascontiguousarray(np.asarray(v, np.float32).reshape(-1, 128).T)


def _bias_table(rel_bias):
    k = np.arange(128)[:, None, None]
    o = np.arange(5)[None, :, None]
    q = np.arange(128)[None, None, :]
    rel = np.clip(q - k + 128 * o, -128, 128) + 128
    invalid = ((o == 0) & (q < 64) & (k >= 64)) | ((o == 4) & (q >= 64) & (k < 64))
    tab = np.asarray(rel_bias, np.float32)[:, rel]
    tab = np.where(invalid[None], np.float32(NEG), tab)
    return np.ascontiguousarray(tab.transpose(1, 0, 2, 3)).reshape(128, 8 * 5 * 128)


_NC_CACHE = {}


def kernel(x, c, w_ada, b_ada, mix_norm_g, w_in, q_norm_g, k_norm_g, rel_bias, gmlp_norm_g,
           w_spatial, b_spatial, attn_out_g, gmlp_out_g, w_out, ff_norm_g, w_ff1, w_ff2):
    f = lambda a: np.ascontiguousarray(np.asarray(a, dtype=np.float32))
    x = f(x)
    c = f(c)
    if "nc" not in _NC_CACHE:
        _NC_CACHE["nc"] = build_nc()
    nc = _NC_CACHE["nc"]
    shared = {
        "w_ada": f(w_ada[0]), "b_ada": f(b_ada[0]).reshape(1, -1),
        "gmix_t": _pk(mix_norm_g[0]), "gff_t": _pk(ff_norm_g[0]),
        "w_in": f(w_in[0]),
        "gq_t": f(q_norm_g[0]).reshape(128, 1), "gk_t": f(k_norm_g[0]).reshape(128, 1),
        "biasT": _bias_table(rel_bias[0]),
        "vg_bc": np.ascontiguousarray(np.broadcast_to(f(gmlp_norm_g[0]).reshape(1, 1024), (128, 1024))),
        "wsT": np.ascontiguousarray(f(w_spatial[0]).transpose(2, 0, 1)).reshape(128, 8 * 128),
        "bs_t": np.ascontiguousarray(f(b_spatial[0]).T),
        "gout_t": _pk(np.concatenate([f(attn_out_g[0]), f(gmlp_out_g[0])])),
        "w_out": f(w_out[0]), "w_ff1": f(w_ff1[0]), "w_ff2": f(w_ff2[0]),
        "ident": np.eye(128, dtype=np.float32),
    }
    in_maps = []
    for b in range(NCORES):
        m = dict(shared)
        m["x"] = x[b]
        m["c_t"] = _pk(c[b])
        in_maps.append(m)
    res = run_bass_kernel_spmd(nc, in_maps, core_ids=list(range(NCORES)))
    return np.stack([np.asarray(r["out"], dtype=np.float32) for r in res.results], axis=0)
```

```python
import numpy as np
from contextlib import ExitStack
import concourse.bass as bass
import concourse.mybir as mybir
from concourse.bass_utils import run_bass_kernel_spmd

F32 = mybir.dt.float32
BF16 = mybir.dt.bfloat16
AF = mybir.ActivationFunctionType
ALU = mybir.AluOpType
AX = mybir.AxisListType

D = 2048
S = 2048
NCORES = 8
T = 512
NT = S // T
EPS = 1e-6
NSLOT = 6
NEG = -30000.0


class Buf:
    __slots__ = ("name", "last_w", "readers", "bank")

    def __init__(self, name, bank=False):
        self.name = name
        self.last_w = None
        self.readers = {}
        self.bank = bank


class DSem:
    __slots__ = ("key", "count")

    def __init__(self, key):
        self.key = key
        self.count = 0


class Prog:
    ENGS = ("pe", "act", "dve", "pool", "sp")

    def __init__(self, nc, same_engine_sync=True):
        self.nc = nc
        self.ops = {e: [] for e in self.ENGS}
        self.cnt = {e: 0 for e in self.ENGS}
        self.waited = {e: {} for e in self.ENGS}
        self.semkeys = list(self.ENGS)
        self.same_engine_sync = same_engine_sync
        self.final_tokens = []

    def dsem(self, name):
        key = "d_%s_%d" % (name, len(self.semkeys))
        self.semkeys.append(key)
        return DSem(key)

    def _deps(self, eng, reads, writes):
        deps = {}

        def add(tok):
            if tok is None:
                return
            k, v = tok
            if deps.get(k, 0) < v:
                deps[k] = v
        for b in reads:
            add(b.last_w)
            if b.bank:
                for k, v in b.readers.items():
                    if k != eng:
                        add((k, v))
        for b in writes:
            add(b.last_w)
            for k, v in b.readers.items():
                add((k, v))
        waits = []
        w = self.waited[eng]
        for k, v in deps.items():
            if k == eng and (eng == "pe" or not self.same_engine_sync):
                continue
            if w.get(k, 0) >= v:
                continue
            w[k] = v
            waits.append((k, v))
        return waits

    def _mark(self, tok, reads, writes):
        k, v = tok
        for b in writes:
            b.last_w = tok
            b.readers = {}
        for b in reads:
            if b.readers.get(k, 0) < v:
                b.readers[k] = v

    def op(self, eng, fn, reads=(), writes=()):
        waits = self._deps(eng, reads, writes)
        self.cnt[eng] += 1
        tok = (eng, self.cnt[eng])
        self.ops[eng].append((waits, fn, (eng, 1)))
        self._mark(tok, reads, writes)
        return tok

    def dma(self, queue, fn, dsem, reads=(), writes=(), final=False):
        waits = self._deps(queue, reads, writes)
        dsem.count += 1
        tok = (dsem.key, 16 * dsem.count)
        self.ops[queue].append((waits, fn, (dsem.key, 16)))
        self._mark(tok, reads, writes)
        if final:
            self.final_tokens.append(tok)
        return tok

    def emit(self):
        nc = self.nc
        with ExitStack() as st:
            sems = {}
            for k in self.semkeys:
                sems[k] = st.enter_context(nc.semaphore(k))
            fin = {}
            for k, v in self.final_tokens:
                fin[k] = max(fin.get(k, 0), v)
            block = st.enter_context(nc.Block())
            handles = {"pe": block.tensor, "act": block.scalar, "dve": block.vector,
                       "pool": block.gpsimd, "sp": block.sync}

            def make(engname):
                oplist = self.ops[engname]

                def body(e):
                    for waits, fn, inc in oplist:
                        for k, v in waits:
                            e.wait_ge(sems[k], v)
                        ins = fn(e)
                        ins.then_inc(sems[inc[0]], inc[1])
                    if engname == "sp":
                        for k, v in fin.items():
                            e.wait_ge(sems[k], v)
                return body
            for engname in self.ENGS:
                handles[engname](make(engname))


class _DryProg:
    def dsem(self, name):
        return DSem(name)

    def op(self, *a, **k):
        return None

    def dma(self, *a, **k):
        return None


def build_nc(ntiles=NT, same_engine_sync=True):
    nc = bass.Bass("TRN2", target_bir_lowering=False)

    def din(name, shape):
        return nc.dram_tensor(name, shape, F32, kind="ExternalInput").ap()
    x = din("x", [S, D])
    c_t = din("c_t", [128, 16])
    w_ada = din("w_ada", [D, 6 * D])
    b_ada = din("b_ada", [1, 6 * D])
    gmix_t = din("gmix_t", [128, 16])
    gff_t = din("gff_t", [128, 16])
    w_in = din("w_in", [D, 5120])
    gq_t = din("gq_t", [128, 1])
    gk_t = din("gk_t", [128, 1])
    biasT = din("biasT", [128, 8 * 5 * 128])
    vg_bc = din("vg_bc", [128, 1024])
    wsT = din("wsT", [128, 8 * 128])
    bs_t = din("bs_t", [128, 8])
    gout_t = din("gout_t", [128, 16])
    w_out = din("w_out", [D, D])
    w_ff1 = din("w_ff1", [D, 4 * D])
    w_ff2 = din("w_ff2", [4 * D, D])
    ident = din("ident", [128, 128])
    out = nc.dram_tensor("out", [S, D], F32, kind="ExternalOutput").ap()

    w_ada_v = w_ada.rearrange("(kc p) n -> p kc n", p=128)
    w_in_v = w_in.rearrange("(kc p) n -> p kc n", p=128)
    w_out_v = w_out.rearrange("(kc p) n -> p kc n", p=128)
    w_ff1_v = w_ff1.rearrange("(kc p) n -> p kc n", p=128)
    w_ff2_v = w_ff2.rearrange("(kc p) n -> p kc n", p=128)

    with ExitStack() as st:
        def sb(name, shape, dt):
            return st.enter_context(nc.sbuf_tensor(name, shape, dt))

        ident_f = sb("ident_f", [128, 128], F32)
        ident_b = sb("ident_b", [128, 128], BF16)
        ones_f = sb("ones_f", [1, 128], F32)
        eps_t = sb("eps_t", [128, 1], F32)
        c_sb = sb("c_sb", [128, 16], F32)
        cond_b = sb("cond_b", [128, 16], BF16)
        gmix_sb = sb("gmix_sb", [128, 16], F32)
        gff_sb = sb("gff_sb", [128, 16], F32)
        gout_sb = sb("gout_sb", [128, 16], F32)
        gq_sb = sb("gq_sb", [128, 1], F32)
        gk_sb = sb("gk_sb", [128, 1], F32)
        bs_sb = sb("bs_sb", [128, 8], F32)
        vg_sb = sb("vg_sb", [128, 1024], F32)
        bias_b = sb("bias_b", [128, 8 * 5 * 128], BF16)
        ws_b = sb("ws_b", [128, 8 * 128], BF16)
        AB = sb("AB", [128, 64], F32)
        gate_bc = [sb("gate_m_bc", [128, D], F32), sb("gate_f_bc", [128, D], F32)]
        rowbuf = [sb("rowbuf%d" % i, [1, 512], F32) for i in range(2)]
        x1 = [sb("x1_%d" % j, [128, D], F32) for j in range(4)]
        actT = sb("actT", [128, 16, T], BF16)
        arena = sb("arena", [128, 16, 512], BF16)
        kTr = sb("kTr", [128, 8, 8 * 128], BF16)
        vring = sb("vring", [128, 8, 8, 130], BF16)
        qT = sb("qT", [128, 8, T], BF16)
        mixj = [sb("mixj%d" % i, [128, D], BF16) for i in range(2)]
        xn = [sb("xn%d" % i, [128, D], BF16) for i in range(2)]
        f32s = sb("f32s", [128, D + 32], F32)
        xst1 = sb("xst1", [128, D], F32)
        NRT = 3
        rtmp = [sb("rtmp%d" % i, [128, 512], F32) for i in range(NRT)]
        eT = [sb("eT%d" % i, [128, 640], BF16) for i in range(2)]
        wslot = [sb("wslot%d" % i, [128, 4, 512], BF16) for i in range(NSLOT)]
        NSTAT = 12
        qkss = sb("qkss", [128, 64], F32)
        qkr = qkss
        stat = [sb("stat%d" % i, [128, 16], F32) for i in range(NSTAT)]
        attn_raw = f32s[:, 0:1040].rearrange("p (h d) -> p h d", h=8)
        gscr = xst1[:].rearrange("p (j n) -> p j n", j=2)

        PS = st.enter_context(nc.psum_tensor("PS", [128, 8, 512], F32))
        PT = [PS[:, 6, :].bitcast(BF16), PS[:, 7, :].bitcast(BF16)]

        def program(P, wsched, dry):
            b_const = Buf("const")
            b_constp = Buf("constp")
            b_c2 = Buf("const2")
            b_AB = [Buf("AB%d" % i) for i in range(4)]
            b_gate = [Buf("gate_m"), Buf("gate_f")]
            b_row = [Buf("row0"), Buf("row1")]
            b_x1 = [Buf("x1_%d" % j) for j in range(4)]
            b_actT = [Buf("actT%d" % j) for j in range(4)]
            b_arena = [Buf("arena%d" % g) for g in range(4)]
            b_kTr = [Buf("kTr%d" % s_) for s_ in range(8)]
            b_vr = [Buf("vr%d" % s_) for s_ in range(8)]
            b_qT = [Buf("qT%d" % j) for j in range(4)]
            b_mixj = [Buf("mixj0"), Buf("mixj1")]
            b_xn = [Buf("xn0"), Buf("xn1")]
            b_f32s = [Buf("f32s")]
            b_xst1 = Buf("xst1")
            b_qkss = Buf("qkss")
            b_rtmp = [Buf("rtmp%d" % i) for i in range(NRT)]
            b_eT = [Buf("eT0"), Buf("eT1")]
            b_ws = [Buf("wslot%d" % i) for i in range(NSLOT)]
            b_stat = [Buf("stat%d" % i) for i in range(NSTAT)]
            b_ps = [[Buf("ps%d" % i, bank=True)] for i in range(8)]

            s_const = P.dsem("const")
            s_constp = P.dsem("constp")
            s_x = [P.dsem("x%d" % j) for j in range(4)]
            s_xs = [P.dsem("xs0"), P.dsem("xs1")]
            s_o = [P.dsem("o%d" % j) for j in range(4)]
            s_ws = [P.dsem("ws%d" % i) for i in range(NSLOT)]
            s_row = [P.dsem("row0"), P.dsem("row1")]

            xst = [(f32s, b_f32s, s_xs[0]), (xst1, [b_xst1], s_xs[1])]

            ctr = {"stat": 0, "rtmp": 0, "ps": 0, "pt": 0, "row": 0, "xn": 0, "xst": 0}

            def nxt(key, n):
                i = ctr[key] % n
                ctr[key] += 1
                return i

            def new_stat():
                i = nxt("stat", NSTAT)
                return stat[i], b_stat[i]

            pinned = set()

            def new_banks(n):
                res = []
                while len(res) < n:
                    b = nxt("ps", 6)
                    if b not in pinned:
                        res.append(b)
                return res

            wstate = {"issued": 0, "consumed": 0}

            def w_issue_upto(i):
                while wstate["issued"] <= i and wstate["issued"] < len(wsched):
                    k = wstate["issued"]
                    s_ = k % NSLOT
                    src = wsched[k]
                    P.dma("pool", (lambda e, s_=s_, src=src: e.dma_start(out=wslot[s_][:], in_=src)),
                          s_ws[s_], writes=[b_ws[s_]])
                    wstate["issued"] += 1

            def w_next(src):
                i = wstate["consumed"]
                wstate["consumed"] += 1
                if dry:
                    wsched.append(src)
                else:
                    w_issue_upto(i + NSLOT - 1)
                s_ = i % NSLOT
                return wslot[s_], b_ws[s_]

            consts = [(ident_f[:], ident), (c_sb[:], c_t), (gmix_sb[:], gmix_t), (gff_sb[:], gff_t),
                      (gout_sb[:], gout_t), (gq_sb[:], gq_t), (gk_sb[:], gk_t), (bs_sb[:], bs_t),
                      (vg_sb[:], vg_bc)]
            for dst, src in consts:
                P.dma("sp", (lambda e, dst=dst, src=src: e.dma_start(out=dst, in_=src)), s_const)
            b_const.last_w = (s_const.key, 16 * s_const.count)
            P.dma("pool", lambda e: e.dma_start(out=bias_b[:], in_=biasT), s_constp)
            P.dma("pool", lambda e: e.dma_start(out=ws_b[:], in_=wsT), s_constp)
            b_constp.last_w = (s_constp.key, 16 * s_constp.count)

            P.op("dve", lambda e: e.memset(eps_t[:], EPS), writes=[b_c2])
            P.op("dve", lambda e: e.memset(ones_f[:], 1.0), writes=[b_c2])
            P.op("dve", lambda e: e.tensor_copy(out=ident_b[:], in_=ident_f[:]), reads=[b_const], writes=[b_c2])
            P.op("dve", lambda e: e.memset(vring[:, :, :, 128:130], 1.0), writes=b_vr)
            P.op("dve", lambda e: e.memset(ws_b[:].rearrange("p (g t) -> p g t", g=8)[64:128, :, 0:64], 0.0),
                 reads=[b_constp], writes=[b_constp])
            bias4 = bias_b[:].rearrange("p (h o q) -> p h o q", h=8, o=5)
            for o_ in (0, 1, 4):
                P.op("dve", (lambda e, o_=o_: e.tensor_tensor(out=bias4[:, :, o_, :], in0=bias4[:, :, o_, :], in1=bias4[:, :, 2, :],
                                                              op=ALU.subtract)), reads=[b_constp], writes=[b_constp])
            P.op("act", lambda e: e.activation(out=cond_b[:], in_=c_sb[:], func=AF.Silu), reads=[b_const], writes=[b_c2])
            P.op("dve", lambda e: e.tensor_scalar(out=gq_sb[:], in0=gq_sb[:], scalar1=float(128 ** -0.5), scalar2=None,
                                                  op0=ALU.mult), reads=[b_const], writes=[b_const])

            def rsqrt_from_ss(ss_ap, n_cols, inv_n, b_in):
                t, bt = new_stat()
                r, br = new_stat()
                P.op("act", lambda e: e.activation(out=t[:, 0:n_cols], in_=ss_ap, func=AF.Ln, scale=inv_n,
                                                   bias=eps_t[:, 0:1]), reads=[b_in, b_c2], writes=[bt])
                P.op("act", lambda e: e.activation(out=r[:, 0:n_cols], in_=t[:, 0:n_cols], func=AF.Exp, scale=-0.5),
                     reads=[bt], writes=[br])
                return r, br

            def ada_block(nb, bk=None):
                ri = nxt("row", 2)
                P.dma("sp", (lambda e, ri=ri, nb=nb: e.dma_start(out=rowbuf[ri][:], in_=b_ada[0:1, nb * 512:(nb + 1) * 512])),
                      s_row[ri], writes=[b_row[ri]])
                if bk is None:
                    bk = new_banks(1)[0]
                for g in range(4):
                    ws, bws = w_next(w_ada_v[:, g * 4:(g + 1) * 4, nb * 512:(nb + 1) * 512])

                    def f(e, ws=ws, g=g, bk=bk):
                        ins = None
                        for k4 in range(4):
                            kc = g * 4 + k4
                            ins = e.matmul(PS[0:1, bk, :], lhsT=cond_b[:, kc:kc + 1], rhs=ws[:, k4, :],
                                           start=(kc == 0), stop=(kc == 15))
                        return ins
                    P.op("pe", f, reads=[bws, b_c2], writes=b_ps[bk])
                P.op("dve", (lambda e, ri=ri, bk=bk: e.tensor_tensor(out=rowbuf[ri][:], in0=PS[0:1, bk, :], in1=rowbuf[ri][:],
                                                                      op=ALU.add)),
                     reads=b_ps[bk] + [b_row[ri]], writes=[b_row[ri]])
                return ri

            def ada_vec_pp(nb0, col0, bk):
                for i in range(4):
                    ri = ada_block(nb0 + i)

                    def f(e, ri=ri, i=i):
                        ins = None
                        for c in range(4):
                            col = col0 + i * 4 + c
                            ins = e.matmul(PS[:, bk, col:col + 1], lhsT=rowbuf[ri][0:1, c * 128:(c + 1) * 128],
                                           rhs=ones_f[0:1, 0:1], start=True, stop=True)
                        return ins
                    P.op("pe", f, reads=[b_row[ri], b_c2], writes=b_ps[bk])

            def ada_mod(which):
                nb_shift, nb_scale = (0, 4) if which == 0 else (12, 16)
                g_sb = gmix_sb if which == 0 else gff_sb
                bk = new_banks(1)[0]
                pinned.add(bk)
                ada_vec_pp(nb_shift, 0, bk)
                ada_vec_pp(nb_scale, 16, bk)
                pinned.discard(bk)
                a0 = which * 32
                P.op("dve", lambda e: e.scalar_tensor_tensor(out=AB[:, a0:a0 + 16], in0=PS[:, bk, 16:32], scalar=1.0,
                                                             in1=g_sb[:], op0=ALU.add, op1=ALU.mult),
                     reads=b_ps[bk] + [b_const], writes=[b_AB[which * 2]])
                P.op("dve", lambda e: e.tensor_copy(out=AB[:, a0 + 16:a0 + 32], in_=PS[:, bk, 0:16]),
                     reads=b_ps[bk], writes=[b_AB[which * 2 + 1]])

            ada_done = set()

            def ada_gate_gen(which, banks=None):
                ada_done.add(which)
                nb0 = 8 if which == 0 else 20
                for i in range(4):
                    ri = ada_block(nb0 + i, None if banks is None else banks[0])
                    bk = new_banks(1)[0] if banks is None else banks[1]
                    P.op("pe", (lambda e, ri=ri, bk=bk: e.matmul(PS[:, bk, :], lhsT=ones_f[0:1, 0:128], rhs=rowbuf[ri][0:1, :],
                                                                 start=True, stop=True)),
                         reads=[b_row[ri], b_c2], writes=b_ps[bk])
                    P.op("dve", (lambda e, i=i, bk=bk: e.tensor_copy(out=gate_bc[which][:, i * 512:(i + 1) * 512], in_=PS[:, bk, :])),
                         reads=b_ps[bk], writes=[b_gate[which]])
                    yield

            def ada_gate(which):
                if which in ada_done:
                    return
                for _ in ada_gate_gen(which):
                    pass

            def _unused_ada_gate(which):
                nb0 = 8 if which == 0 else 20
                for i in range(4):
                    ri = ada_block(nb0 + i)
                    bk = new_banks(1)[0]
                    P.op("pe", (lambda e, ri=ri, bk=bk: e.matmul(PS[:, bk, :], lhsT=ones_f[0:1, 0:128], rhs=rowbuf[ri][0:1, :],
                                                                 start=True, stop=True)),
                         reads=[b_row[ri], b_c2], writes=b_ps[bk])
                    P.op("dve", (lambda e, i=i, bk=bk: e.tensor_copy(out=gate_bc[which][:, i * 512:(i + 1) * 512], in_=PS[:, bk, :])),
                         reads=b_ps[bk], writes=[b_gate[which]])

            def AB_or(acol, kc):
                if acol == "gout":
                    return gout_sb[:, kc:kc + 1]
                return AB[:, acol + kc:acol + kc + 1]

            def transpose_half(src, b_src, j, half, acol, bcol, b_scal):
                pi = nxt("pt", 2)

                def f(e, pi=pi, half=half):
                    ins = None
                    for k in range(8):
                        kc = half * 8 + k
                        ins = e.transpose(out=PT[pi][:, k * 128:(k + 1) * 128], in_=src[:, kc * 128:(kc + 1) * 128],
                                          identity=ident_b[:])
                    return ins
                P.op("pe", f, reads=[b_src, b_c2], writes=b_ps[6 + pi])
                for k in range(8):
                    kc = half * 8 + k
                    dst = actT[:, kc, j * 128:(j + 1) * 128]
                    srcp = PT[pi][:, k * 128:(k + 1) * 128]
                    if pi == 0:
                        if bcol is None:
                            fn = (lambda e, dst=dst, srcp=srcp, kc=kc: e.activation(out=dst, in_=srcp, func=AF.Identity,
                                                                                    scale=AB_or(acol, kc)))
                        else:
                            fn = (lambda e, dst=dst, srcp=srcp, kc=kc: e.activation(out=dst, in_=srcp, func=AF.Identity,
                                                                                    scale=AB_or(acol, kc), bias=AB[:, bcol + kc:bcol + kc + 1]))
                        P.op("act", fn, reads=b_ps[6 + pi] + b_scal, writes=[b_actT[j]])
                    else:
                        if bcol is None:
                            fn = (lambda e, dst=dst, srcp=srcp, kc=kc: e.tensor_scalar(out=dst, in0=srcp, scalar1=AB_or(acol, kc),
                                                                                       scalar2=None, op0=ALU.mult))
                        else:
                            fn = (lambda e, dst=dst, srcp=srcp, kc=kc: e.tensor_scalar(out=dst, in0=srcp, scalar1=AB_or(acol, kc),
                                                                                       scalar2=AB[:, bcol + kc:bcol + kc + 1],
                                                                                       op0=ALU.mult, op1=ALU.add))
                        P.op("dve", fn, reads=b_ps[6 + pi] + b_scal, writes=[b_actT[j]])

            xn_pool = [(xn[0], b_xn[0]), (xn[1], b_xn[1]), (mixj[0], b_mixj[0]), (mixj[1], b_mixj[1])]

            def norm_elem(src_ap, b_src):
                ss, bss = new_stat()
                xnb, bxn = xn_pool[nxt("xn", 4)]
                P.op("act", lambda e: e.activation(out=xnb[:], in_=src_ap, func=AF.Square, accum_out=ss[:, 0:1]),
                     reads=b_src, writes=[bss, bxn])
                r, br = rsqrt_from_ss(ss[:, 0:1], 1, 1.0 / D, bss)
                P.op("dve", lambda e: e.tensor_scalar(out=xnb[:], in0=src_ap, scalar1=r[:, 0:1], scalar2=None, op0=ALU.mult),
                     reads=b_src + [br], writes=[bxn])
                return xnb, bxn

            def norm_transpose_gen(src_ap, b_src, j, a0):
                xnb, bxn = norm_elem(src_ap, b_src)
                yield
                scal = [b_AB[a0 // 32 * 2], b_AB[a0 // 32 * 2 + 1]]
                for half in range(2):
                    transpose_half(xnb, bxn, j, half, a0, a0 + 16, scal)
                    yield

            def norm_transpose(src_ap, b_src, j, a0):
                for _ in norm_transpose_gen(src_ap, b_src, j, a0):
                    pass

            def n1_units(it):
                t0 = it * T
                gens = []
                for j in range(4):
                    def g(j=j):
                        r0 = t0 + j * 128
                        xi = nxt("xst", 2)
                        buf, bb, sem = xst[xi]
                        P.dma("sp", (lambda e: e.dma_start(out=buf[:, 0:D], in_=x[r0:r0 + 128, :])), sem, writes=bb)
                        yield from norm_transpose_gen(buf[:, 0:D], bb, j, 0)
                    gens.append(g())
                return gens

            def run_norm_units(gens):
                for g in gens:
                    next(g, None)
                for g in gens:
                    for _ in g:
                        pass

            def n1(it):
                run_norm_units(n1_units(it))

            def mm_group(srcs, lhs_fn, b_lhs_fn, evac_fn, banks=None):
                if banks is None:
                    banks = new_banks(4)
                nk = len(srcs) * 4
                for g, src in enumerate(srcs):
                    ws, bws = w_next(src)
                    for j in range(4):
                        def f(e, ws=ws, g=g, j=j):
                            ins = None
                            for k4 in range(4):
                                kc = g * 4 + k4
                                ins = e.matmul(PS[:, banks[j], :], lhsT=lhs_fn(kc, j), rhs=ws[:, k4, :],
                                               start=(kc == 0), stop=(kc == nk - 1))
                            return ins
                        P.op("pe", f, reads=[bws] + b_lhs_fn(g, j), writes=b_ps[banks[j]])
                        if g == len(srcs) - 1:
                            evac_fn(j, banks[j])
                        yield

            def run(gen):
                for _ in gen:
                    pass

            def interleave(main, side, ratio):
                for _ in main:
                    for _ in range(ratio):
                        next(side, None)
                for _ in side:
                    pass

            def w_in_srcs(nb):
                return [w_in_v[:, g * 4:(g + 1) * 4, nb * 512:(nb + 1) * 512] for g in range(4)]

            qk = arena[:].rearrange("p (j a) n -> p j (a n)", a=4)
            uz = qk

            def tile_body(it):
                t0 = it * T
                for j in range(4):
                    r0 = t0 + j * 128
                    P.dma("sp", (lambda e, j=j, r0=r0: e.dma_start(out=x1[j][:], in_=x[r0:r0 + 128, :])), s_x[j], writes=[b_x1[j]])
                if it == 0:
                    for g in n1_first:
                        for _ in g:
                            pass

                hT_fn = lambda kc, j: actT[:, kc, j * 128:(j + 1) * 128]
                hT_b = lambda g, j: [b_actT[j]]

                def evac_qk(nb):
                    def evac(j, bk):
                        dst = qk[:, j, nb * 512:(nb + 1) * 512]
                        if nb % 2 == 0:
                            P.op("act", lambda e: e.activation(out=dst, in_=PS[:, bk, :], func=AF.Copy),
                                 reads=b_ps[bk], writes=[b_arena[j]])
                        else:
                            P.op("dve", lambda e: e.tensor_copy(out=dst, in_=PS[:, bk, :]), reads=b_ps[bk], writes=[b_arena[j]])
                    return evac
                for nb in range(4):
                    run(mm_group(w_in_srcs(nb), hT_fn, hT_b, evac_qk(nb)))

                def evac_v(nb):
                    def evac(j, bk):
                        sl = (it * 4 + j) % 8
                        h0 = (nb - 4) * 4
                        dst = vring[:, sl, h0:h0 + 4, 0:128]
                        srcp = PS[:, bk, :].rearrange("p (h d) -> p h d", h=4)
                        P.op("act", lambda e: e.activation(out=dst, in_=srcp, func=AF.Copy), reads=b_ps[bk], writes=[b_vr[sl]])
                    return evac
                for j in range(4):
                    qkj = qk[:, j, :]
                    P.op("dve", (lambda e, qkj=qkj: e.tensor_tensor(out=f32s[:, 0:D], in0=qkj, in1=qkj, op=ALU.mult)),
                         reads=[b_arena[j]], writes=b_f32s)
                    P.op("dve", (lambda e, j=j: e.tensor_reduce(out=qkss[:, j * 16:(j + 1) * 16],
                                                                in_=f32s[:, 0:D].rearrange("p (h d) -> p h d", h=16),
                                                                axis=AX.X, op=ALU.add)),
                         reads=b_f32s, writes=[b_qkss])
                run(mm_group(w_in_srcs(4), hT_fn, hT_b, evac_v(4)))
                P.op("act", lambda e: e.activation(out=qkr[:], in_=qkss[:], func=AF.Ln, scale=1.0 / 128, bias=eps_t[:, 0:1]),
                     reads=[b_qkss, b_c2], writes=[b_qkss])
                P.op("act", lambda e: e.activation(out=qkr[:], in_=qkr[:], func=AF.Exp, scale=-0.5), reads=[b_qkss], writes=[b_qkss])
                for j in range(4):
                    qkj = qk[:, j, :]
                    P.op("dve", (lambda e, qkj=qkj, j=j: e.tensor_tensor(
                        out=qkj.rearrange("p (h d) -> p h d", h=16), in0=qkj.rearrange("p (h d) -> p h d", h=16),
                        in1=qkr[:, j * 16:(j + 1) * 16].unsqueeze(2).to_broadcast([128, 16, 128]), op=ALU.mult)),
                        reads=[b_arena[j], b_qkss], writes=[b_arena[j]])
                run(mm_group(w_in_srcs(5), hT_fn, hT_b, evac_v(5)))

                for j in range(4):
                    sl = (it * 4 + j) % 8
                    qkj = qk[:, j, :]
                    for half in range(2):
                        pi = nxt("pt", 2)

                        def f(e, pi=pi, half=half, qkj=qkj):
                            ins = None
                            for h in range(8):
                                c0 = half * 1024 + h * 128
                                ins = e.transpose(out=PT[pi][:, h * 128:(h + 1) * 128], in_=qkj[:, c0:c0 + 128],
                                                  identity=ident_b[:])
                            return ins
                        P.op("pe", f, reads=[b_arena[j], b_c2], writes=b_ps[6 + pi])
                        srcp = PT[pi].rearrange("p (h t) -> p h t", h=8)
                        dst = qT[:, :, j * 128:(j + 1) * 128] if half == 0 else kTr[:, :, sl * 128:(sl + 1) * 128]
                        gsc = gq_sb if half == 0 else gk_sb
                        bdst = [b_qT[j]] if half == 0 else [b_kTr[sl]]
                        if pi == 0:
                            P.op("act", (lambda e, dst=dst, srcp=srcp, gsc=gsc: e.activation(out=dst, in_=srcp, func=AF.Identity,
                                                                                             scale=gsc[:, 0:1])),
                                 reads=b_ps[6 + pi] + [b_const], writes=bdst)
                        else:
                            P.op("dve", (lambda e, dst=dst, srcp=srcp, gsc=gsc: e.tensor_scalar(out=dst, in0=srcp, scalar1=gsc[:, 0:1],
                                                                                                scalar2=None, op0=ALU.mult)),
                                 reads=b_ps[6 + pi] + [b_const], writes=bdst)

                def evac_uz(nb):
                    def evac(j, bk):
                        dst = uz[:, j, (nb - 6) * 512:(nb - 5) * 512]
                        P.op("act", lambda e: e.activation(out=dst, in_=PS[:, bk, :], func=AF.Gelu),
                             reads=b_ps[bk], writes=[b_arena[j]])
                    return evac
                for nb in range(6, 10):
                    run(mm_group(w_in_srcs(nb), hT_fn, hT_b, evac_uz(nb)))

                def attn_block(j):
                    gt = it * 4 + j
                    kbs = [kb for kb in range(gt - 4, gt + 1) if kb >= 0]
                    nb_k = len(kbs)
                    mi = j % 2

                    def scores(h):
                        s_ = h % 2

                        def sc_ap(i):
                            if i < 4:
                                return PS[:, 2 * s_, i * 128:(i + 1) * 128]
                            return PS[:, 2 * s_ + 1, 0:128]

                        def f(e):
                            ins = None
                            for i, kb in enumerate(kbs):
                                o = gt - kb
                                sl = kb % 8
                                nobias = o in (2, 3)
                                ins = e.matmul(sc_ap(i), lhsT=kTr[:, h, sl * 128:(sl + 1) * 128], rhs=qT[:, h, j * 128:(j + 1) * 128],
                                               start=True, stop=nobias)
                                if not nobias:
                                    b0 = (h * 5 + o) * 128
                                    ins = e.matmul(sc_ap(i), lhsT=ident_b[:], rhs=bias_b[:, b0:b0 + 128], start=False, stop=True)
                            return ins
                        P.op("pe", f, reads=[b_kTr[kb % 8] for kb in kbs] + [b_qT[j], b_constp, b_c2],
                             writes=b_ps[2 * s_] + b_ps[2 * s_ + 1])
                        n4 = min(nb_k, 4)
                        P.op("act", (lambda e: e.activation(out=eT[s_][:, 0:n4 * 128], in_=PS[:, 2 * s_, 0:n4 * 128], func=AF.Exp)),
                             reads=b_ps[2 * s_], writes=[b_eT[s_]])
                        if nb_k == 5:
                            P.op("act", (lambda e: e.activation(out=eT[s_][:, 512:640], in_=PS[:, 2 * s_ + 1, 0:128], func=AF.Exp)),
                                 reads=b_ps[2 * s_ + 1], writes=[b_eT[s_]])

                    def pv_(h):
                        s_ = h % 2
                        pv = PS[:, 2 * s_ + 1, 256:256 + 129]

                        def f2(e):
                            ins = None
                            for i, kb in enumerate(kbs):
                                sl = kb % 8
                                ins = e.matmul(pv, lhsT=eT[s_][:, i * 128:(i + 1) * 128], rhs=vring[:, sl, h, 0:129],
                                               start=(i == 0), stop=(i == len(kbs) - 1))
                            return ins
                        P.op("pe", f2, reads=[b_eT[s_]] + [b_vr[kb % 8] for kb in kbs], writes=b_ps[2 * s_ + 1])
                        P.op("act", (lambda e: e.activation(out=attn_raw[:, h, 0:129], in_=pv, func=AF.Copy)),
                             reads=b_ps[2 * s_ + 1], writes=b_f32s)
                    for h in range(9):
                        if h < 8:
                            scores(h)
                        if h >= 1:
                            pv_(h - 1)
                        yield
                    rd, brd = new_stat()
                    P.op("dve", (lambda e: e.reciprocal(out=rd[:, 0:8], in_=attn_raw[:, :, 128])), reads=b_f32s, writes=[brd])
                    P.op("dve", (lambda e: e.tensor_tensor(out=attn_raw[:, :, 0:128], in0=attn_raw[:, :, 0:128],
                                                           in1=rd[:, 0:8].unsqueeze(2).to_broadcast([128, 8, 128]), op=ALU.mult)),
                         reads=b_f32s + [brd], writes=b_f32s)
                    ss, bss = new_stat()
                    mix3 = mixj[mi][:, 0:1024].rearrange("p (h d) -> p h d", h=8)
                    P.op("act", (lambda e: e.activation(out=mix3, in_=attn_raw[:, :, 0:128], func=AF.Square, accum_out=ss[:, 0:1])),
                         reads=b_f32s, writes=[bss, b_mixj[mi]])
                    r, br = rsqrt_from_ss(ss[:, 0:1], 1, 1.0 / 1024, bss)
                    P.op("dve", (lambda e: e.tensor_scalar(out=mix3, in0=attn_raw[:, :, 0:128], scalar1=r[:, 0:1],
                                                           scalar2=None, op0=ALU.mult)),
                         reads=b_f32s + [br], writes=[b_mixj[mi]])
                    yield

                def gmlp_pair(j0):
                    zz = uz[:, j0:j0 + 2, 1024:2048]
                    gu = uz[:, j0:j0 + 2, 0:1024]
                    ba = [b_arena[j0], b_arena[j0 + 1]]
                    P.op("dve", (lambda e: e.tensor_tensor(out=gscr, in0=zz, in1=zz, op=ALU.mult)), reads=ba, writes=[b_xst1])
                    ss, bss = new_stat()
                    P.op("dve", (lambda e: e.tensor_reduce(out=ss[:, 0:16], in_=xst1[:].rearrange("p (h d) -> p h d", h=16),
                                                           axis=AX.X, op=ALU.add)), reads=[b_xst1], writes=[bss])
                    yield
                    yield
                    yield
                    r, br = rsqrt_from_ss(ss[:, 0:16], 16, 1.0 / 128, bss)
                    yield
                    P.op("dve", (lambda e: e.tensor_tensor(
                        out=xst1[:].rearrange("p (j h d) -> p j h d", j=2, h=8), in0=zz.rearrange("p j (h d) -> p j h d", h=8),
                        in1=r[:, 0:16].rearrange("p (j h) -> p j h", j=2).unsqueeze(3).to_broadcast([128, 2, 8, 128]), op=ALU.mult)),
                        reads=ba + [br], writes=[b_xst1])
                    yield
                    P.op("dve", (lambda e: e.tensor_tensor(out=zz, in0=gscr, in1=vg_sb[:].unsqueeze(1).to_broadcast([128, 2, 1024]),
                                                           op=ALU.mult)), reads=[b_xst1, b_const], writes=ba)
                    yield

                    def f3(e):
                        ins = None
                        for jj in range(2):
                            for g in range(8):
                                ins = e.matmul(PS[:, 4 + 2 * jj + g // 4, (g % 4) * 128:(g % 4 + 1) * 128],
                                               lhsT=ws_b[:, g * 128:(g + 1) * 128],
                                               rhs=uz[:, j0 + jj, 1024 + g * 128:1024 + (g + 1) * 128], start=True, stop=True)
                        return ins
                    P.op("pe", f3, reads=ba + [b_constp], writes=b_ps[4] + b_ps[5] + b_ps[6] + b_ps[7])
                    for jj in range(2):
                        for hh in range(2):
                            P.op("dve", (lambda e, jj=jj, hh=hh: e.tensor_tensor(
                                out=gscr[:, jj, hh * 512:(hh + 1) * 512].rearrange("p (g c) -> p g c", g=4),
                                in0=PS[:, 4 + 2 * jj + hh, :].rearrange("p (g c) -> p g c", g=4),
                                in1=bs_sb[:, hh * 4:(hh + 1) * 4].unsqueeze(2).to_broadcast([128, 4, 128]), op=ALU.add)),
                                reads=b_ps[4 + 2 * jj + hh] + [b_const], writes=[b_xst1])
                    yield
                    P.op("dve", (lambda e: e.tensor_tensor(out=gscr, in0=gscr, in1=gu, op=ALU.mult)),
                         reads=[b_xst1] + ba, writes=[b_xst1])
                    yield
                    yield
                    yield
                    yield
                    ss2, bss2 = new_stat()
                    for jj in range(2):
                        mi = (j0 + jj) % 2
                        P.op("act", (lambda e, jj=jj, mi=mi: e.activation(out=mixj[mi][:, 1024:2048], in_=gscr[:, jj, :], func=AF.Square,
                                                                          accum_out=ss2[:, jj:jj + 1])),
                             reads=[b_xst1], writes=[bss2, b_mixj[mi]])
                    r2, br2 = rsqrt_from_ss(ss2[:, 0:2], 2, 1.0 / 1024, bss2)
                    yield
                    for jj in range(2):
                        mi = (j0 + jj) % 2
                        P.op("dve", (lambda e, jj=jj, mi=mi: e.tensor_scalar(out=mixj[mi][:, 1024:2048], in0=gscr[:, jj, :],
                                                                             scalar1=r2[:, jj:jj + 1], scalar2=None, op0=ALU.mult)),
                             reads=[b_xst1, br2], writes=[b_mixj[mi]])
                    yield

                def mix_T(j):
                    for half in range(2):
                        transpose_half(mixj[j % 2], b_mixj[j % 2], j, half, "gout", None, [b_const])
                        yield

                def chain(*gens):
                    for g in gens:
                        yield from g

                def interleave_every(main, side, every):
                    n = 0
                    for _ in main:
                        n += 1
                        if n % every == 0:
                            next(side, None)

                def roundrobin(ga, gb, ka, kb):
                    alive = [True, True]
                    while alive[0] or alive[1]:
                        for gi, (g, k) in enumerate(((ga, ka), (gb, kb))):
                            for _ in range(k):
                                if alive[gi]:
                                    try:
                                        next(g)
                                    except StopIteration:
                                        alive[gi] = False
                                        continue
                                    yield
                sideA = gmlp_pair(0)
                sideB = chain(mix_T(0), mix_T(1), gmlp_pair(2))
                if it == 0:
                    sideA = roundrobin(sideA, ada_gate_gen(0, banks=(4, 5)), 4, 1)
                    sideB = roundrobin(sideB, ada_gate_gen(1, banks=(4, 5)), 5, 1)
                interleave_every(chain(attn_block(0), attn_block(1)), sideA, 1)
                run(sideA)
                interleave_every(chain(attn_block(2), attn_block(3)), sideB, 1)
                run(sideB)
                run(chain(mix_T(2), mix_T(3)))

                if it == 0:
                    ada_gate(0)

                def evac_res(which, nb):
                    def evac(j, bk):
                        P.op("dve", (lambda e, bk=bk: e.tensor_tensor(out=PS[:, bk, :], in0=PS[:, bk, :],
                                                                      in1=gate_bc[which][:, nb * 512:(nb + 1) * 512], op=ALU.mult)),
                             reads=[b_gate[which]], writes=b_ps[bk])
                        P.op("dve", (lambda e, bk=bk, j=j: e.tensor_tensor(out=x1[j][:, nb * 512:(nb + 1) * 512],
                                                                           in0=PS[:, bk, :], in1=x1[j][:, nb * 512:(nb + 1) * 512], op=ALU.add)),
                             reads=b_ps[bk] + [b_x1[j]], writes=[b_x1[j]])
                    return evac
                for nb in range(4):
                    srcs = [w_out_v[:, g * 4:(g + 1) * 4, nb * 512:(nb + 1) * 512] for g in range(4)]
                    run(mm_group(srcs, hT_fn, hT_b, evac_res(0, nb)))

                if it == 0:
                    ada_mod(1)
                run_norm_units([norm_transpose_gen(x1[j][:], [b_x1[j]], j, 32) for j in range(4)])

                for hf in range(4):
                    for hb in range(4):
                        c0 = hf * 2048 + hb * 512
                        banks = new_banks(4)
                        for g in range(4):
                            ws, bws = w_next(w_ff1_v[:, g * 4:(g + 1) * 4, c0:c0 + 512])
                            for q4 in range(4):
                                def f(e, ws=ws, g=g, q4=q4, banks=banks):
                                    ins = None
                                    for k4 in range(4):
                                        kc = g * 4 + k4
                                        ins = e.matmul(PS[:, banks[q4], :], lhsT=ws[:, k4, q4 * 128:(q4 + 1) * 128], rhs=actT[:, kc, :],
                                                       start=(kc == 0), stop=(kc == 15))
                                    return ins
                                P.op("pe", f, reads=[bws] + b_actT, writes=b_ps[banks[q4]])
                        for q4 in range(4):
                            ti = nxt("rtmp", NRT)
                            bk = banks[q4]
                            P.op("act", (lambda e, ti=ti, bk=bk: e.activation(out=rtmp[ti][:], in_=PS[:, bk, :], func=AF.Relu)),
                                 reads=b_ps[bk], writes=[b_rtmp[ti]])
                            P.op("dve", (lambda e, ti=ti, hb=hb, q4=q4: e.tensor_tensor(out=arena[:, hb * 4 + q4, :], in0=rtmp[ti][:],
                                                                                        in1=rtmp[ti][:], op=ALU.mult)),
                                 reads=[b_rtmp[ti]], writes=[b_arena[hb]])
                    if it == 0 and hf == 0:
                        ada_gate(1)

                    def ff2_quarter(hf=hf):
                        for nb in range(4):
                            srcs = [w_ff2_v[:, hf * 16 + g * 4:hf * 16 + g * 4 + 4, nb * 512:(nb + 1) * 512] for g in range(4)]
                            yield from mm_group(srcs, lambda kc, j: arena[:, kc, j * 128:(j + 1) * 128], lambda g, j: [b_arena[g]],
                                                evac_res(1, nb))
                    if hf == 3 and it + 1 < ntiles:
                        g0, g1, g2, g3 = n1_units(it + 1)
                        order = [g0, g1, None, None, g0, g0, g2, g1, g1, g3, g2, g2, None, g3, g3]

                        def side():
                            for g in order:
                                if g is not None:
                                    next(g, None)
                                yield
                        sd = side()
                        interleave_every(ff2_quarter(), sd, 4)
                        run(sd)
                    else:
                        run(ff2_quarter())
                for j in range(4):
                    r0 = t0 + j * 128
                    P.dma("sp", (lambda e, j=j, r0=r0: e.dma_start(out=out[r0:r0 + 128, :], in_=x1[j][:])), s_o[j],
                          reads=[b_x1[j]], final=True)

            n1_first = n1_units(0)
            for g in n1_first:
                next(g, None)
            ada_mod(0)
            for it in range(ntiles):
                tile_body(it)
            if not dry:
                assert wstate["consumed"] == len(wsched), (wstate, len(wsched))

        wsched = []
        program(_DryProg(), wsched, True)
        P = Prog(nc, same_engine_sync=same_engine_sync)
        program(P, wsched, False)
        P.emit()
    return nc


def _pk(v):
    return np.ascontiguousarray(np.asarray(v, np.float32).reshape(-1, 128).T)


def _bias_table(rel_bias):
    k = np.arange(128)[:, None, None]
    o = np.arange(5)[None, :, None]
    q = np.arange(128)[None, None, :]
    rel = np.clip(q - k + 128 * o, -128, 128) + 128
    invalid = ((o == 0) & (q < 64) & (k >= 64)) | ((o == 4) & (q >= 64) & (k < 64))
    tab = np.asarray(rel_bias, np.float32)[:, rel]
    tab = np.where(invalid[None], np.float32(NEG), tab)
    return np.ascontiguousarray(tab.transpose(1, 0, 2, 3)).reshape(128, 8 * 5 * 128)


_NC_CACHE = {}


def kernel(x, c, w_ada, b_ada, mix_norm_g, w_in, q_norm_g, k_norm_g, rel_bias, gmlp_norm_g,
           w_spatial, b_spatial, attn_out_g, gmlp_out_g, w_out, ff_norm_g, w_ff1, w_ff2):
    f = lambda a: np.ascontiguousarray(np.asarray(a, dtype=np.float32))
    x = f(x)
    c = f(c)
    if "nc" not in _NC_CACHE:
        _NC_CACHE["nc"] = build_nc()
    nc = _NC_CACHE["nc"]
    shared = {
        "w_ada": f(w_ada[0]), "b_ada": f(b_ada[0]).reshape(1, -1),
        "gmix_t": _pk(mix_norm_g[0]), "gff_t": _pk(ff_norm_g[0]),
        "w_in": f(w_in[0]),
        "gq_t": f(q_norm_g[0]).reshape(128, 1), "gk_t": f(k_norm_g[0]).reshape(128, 1),
        "biasT": _bias_table(rel_bias[0]),
        "vg_bc": np.ascontiguousarray(np.broadcast_to(f(gmlp_norm_g[0]).reshape(1, 1024), (128, 1024))),
        "wsT": np.ascontiguousarray(f(w_spatial[0]).transpose(2, 0, 1)).reshape(128, 8 * 128),
        "bs_t": np.ascontiguousarray(f(b_spatial[0]).T),
        "gout_t": _pk(np.concatenate([f(attn_out_g[0]), f(gmlp_out_g[0])])),
        "w_out": f(w_out[0]), "w_ff1": f(w_ff1[0]), "w_ff2": f(w_ff2[0]),
        "ident": np.eye(128, dtype=np.float32),
    }
    in_maps = []
    for b in range(NCORES):
        m = dict(shared)
        m["x"] = x[b]
        m["c_t"] = _pk(c[b])
        in_maps.append(m)
    res = run_bass_kernel_spmd(nc, in_maps, core_ids=list(range(NCORES)))
    return np.stack([np.asarray(r["out"], dtype=np.float32) for r in res.results], axis=0)
```
